# Optimizing a Trainium2 kernel written in Bass

```python
import math
import jax, jax.numpy as jnp
from jax import lax
import numpy as np

D_MODEL = 2048
BATCH = 16
SEQ = 256
DEPTH = 1
DEC_BATCH = 4
DEC_SEQ = 2048
PAST_LEN = 256

GRID_W = 64
ATT_HEAD_DIM = 64
V_HEAD_DIM = 2 * ATT_HEAD_DIM
ATT_WIDTH = D_MODEL // 2
N_ATT_HEADS = ATT_WIDTH // V_HEAD_DIM
QK_WIDTH = N_ATT_HEADS * 2 * ATT_HEAD_DIM
SGU_WIDTH = D_MODEL // 2
SGU_GROUP_DIM = 128
N_SGU_GROUPS = SGU_WIDTH // SGU_GROUP_DIM
CHUNK = 128
MIX_WIDTH = ATT_WIDTH + SGU_WIDTH
IN_WIDTH = 2 * QK_WIDTH + ATT_WIDTH + 2 * SGU_WIDTH
D_FF = 4 * D_MODEL
Q_BLOCK = 128
ROPE_BASE = 10000.0
EPS = 1e-6

kernel_name = "hybrid_diffattn_sgu_dit_step"


def rms_norm(x, w):
    xf = x.astype(jnp.float32)
    y = xf * lax.rsqrt(jnp.mean(xf * xf, axis=-1, keepdims=True) + EPS)
    return (y * w.astype(jnp.float32)).astype(x.dtype)


def layer_norm(x, w):
    xf = x.astype(jnp.float32)
    mu = jnp.mean(xf, axis=-1, keepdims=True)
    xc = xf - mu
    y = xc * lax.rsqrt(jnp.mean(xc * xc, axis=-1, keepdims=True) + EPS)
    return (y * w.astype(jnp.float32)).astype(x.dtype)


def grid_rope_tables(n, dtype):
    rows = n // GRID_W
    r, col = jnp.meshgrid(jnp.arange(rows), jnp.arange(GRID_W), indexing="ij")
    r = r.reshape(-1).astype(jnp.float32)
    col = col.reshape(-1).astype(jnp.float32)
    n_freq = ATT_HEAD_DIM // 4
    freqs = ROPE_BASE ** (-jnp.arange(n_freq, dtype=jnp.float32) / n_freq)
    ang_r = r[:, None] * freqs
    ang_c = col[:, None] * freqs
    ang = jnp.concatenate([ang_r, ang_r, ang_c, ang_c], axis=-1)
    return jnp.cos(ang).astype(dtype), jnp.sin(ang).astype(dtype)


def apply_rope(x, cos, sin):
    q = ATT_HEAD_DIM // 4
    a1, a2, b1, b2 = jnp.split(x, [q, 2 * q, 3 * q], axis=-1)
    rot = jnp.concatenate([-a2, a1, -b2, b1], axis=-1)
    c = cos[None, :, None, None, :]
    s = sin[None, :, None, None, :]
    return x * c + rot * s


def diff_attention(q, k, v, lam, subln_w, lam_init):
    b, nq = q.shape[0], q.shape[1]
    nblk = nq // Q_BLOCK
    qb = q.reshape(b, nblk, Q_BLOCK, N_ATT_HEADS, 2, ATT_HEAD_DIM).swapaxes(0, 1)
    scale = ATT_HEAD_DIM ** -0.5
    lam32 = lam.astype(jnp.float32)

    def block(qi):
        s = jnp.einsum("bqhcd,bkhcd->bhcqk", qi, k,
                       preferred_element_type=jnp.float32) * scale
        p = jax.nn.softmax(s, axis=-1)
        a = p[:, :, 0] - lam32 * p[:, :, 1]
        return jnp.einsum("bhqk,bkhe->bqhe", a.astype(v.dtype), v)

    o = lax.map(block, qb)
    o = o.swapaxes(0, 1).reshape(b, nq, N_ATT_HEADS, V_HEAD_DIM)
    o = rms_norm(o, subln_w) * (1.0 - lam_init)
    return o.reshape(b, nq, ATT_WIDTH)


def spatial_gating(u, g, norm_w, w_s, b_s):
    b, n, _ = u.shape
    u = jax.nn.gelu(u)
    g = layer_norm(jax.nn.gelu(g), norm_w)
    gc = g.reshape(b, n // CHUNK, CHUNK, N_SGU_GROUPS, SGU_GROUP_DIM)
    mixed = jnp.einsum("gpq,bcqgd->bcpgd", w_s, gc) + b_s.T[None, None, :, :, None]
    return u * mixed.reshape(b, n, SGU_WIDTH)


def adaln(c, w_ada, b_ada):
    mod = jax.nn.silu(c) @ w_ada + b_ada
    return [m[:, None, :] for m in jnp.split(mod, 6, axis=-1)]


def modulate(x, w, shift, scale):
    return rms_norm(x, w) * (1.0 + scale) + shift


def project_mixers(h, w_in, q_norm_w, k_norm_w):
    b, n, _ = h.shape
    z = h @ w_in
    q, k, v, u, g = jnp.split(
        z, [QK_WIDTH, 2 * QK_WIDTH, 2 * QK_WIDTH + ATT_WIDTH,
            2 * QK_WIDTH + ATT_WIDTH + SGU_WIDTH], axis=-1)
    q = rms_norm(q.reshape(b, n, N_ATT_HEADS, 2, ATT_HEAD_DIM), q_norm_w)
    k = rms_norm(k.reshape(b, n, N_ATT_HEADS, 2, ATT_HEAD_DIM), k_norm_w)
    v = v.reshape(b, n, N_ATT_HEADS, V_HEAD_DIM)
    return q, k, v, u, g


def sq_relu_mlp(h, w1, w2):
    return jnp.square(jax.nn.relu(h @ w1)) @ w2


def setup_inputs(seed: int = 0) -> dict:
    key = jax.random.key(seed)
    ks = jax.random.split(key, 24)
    f32 = jnp.float32
    nrm = lambda k, s, sc: jax.random.normal(k, s, f32) * sc
    L, D = DEPTH, D_MODEL
    return {
        "x_prompt": nrm(ks[0], (BATCH, SEQ, D), 1.0),
        "x_sample": nrm(ks[1], (DEC_BATCH, DEC_SEQ, D), 1.0),
        "cache_k": nrm(ks[2], (DEC_BATCH, L, PAST_LEN, N_ATT_HEADS, 2, ATT_HEAD_DIM), 1.0),
        "cache_v": nrm(ks[3], (DEC_BATCH, L, PAST_LEN, N_ATT_HEADS, V_HEAD_DIM), 1.0),
        "c": nrm(ks[4], (DEC_BATCH, D), 1.0),
        "c_ctx": nrm(ks[5], (D,), 1.0),
        "w_ada": nrm(ks[6], (L, D, 6 * D), 0.5 * D ** -0.5),
        "b_ada": nrm(ks[7], (L, 6 * D), 0.02),
        "norm1_w": 1.0 + nrm(ks[8], (L, D), 0.02),
        "norm2_w": 1.0 + nrm(ks[9], (L, D), 0.02),
        "w_in": nrm(ks[10], (L, D, IN_WIDTH), D ** -0.5),
        "q_norm_w": 1.0 + nrm(ks[11], (L, ATT_HEAD_DIM), 0.02),
        "k_norm_w": 1.0 + nrm(ks[12], (L, ATT_HEAD_DIM), 0.02),
        "lambda_q1": nrm(ks[13], (L, ATT_HEAD_DIM), 0.1),
        "lambda_k1": nrm(ks[14], (L, ATT_HEAD_DIM), 0.1),
        "lambda_q2": nrm(ks[15], (L, ATT_HEAD_DIM), 0.1),
        "lambda_k2": nrm(ks[16], (L, ATT_HEAD_DIM), 0.1),
        "subln_w": 1.0 + nrm(ks[17], (L, V_HEAD_DIM), 0.02),
        "sgu_norm_w": 1.0 + nrm(ks[18], (L, SGU_WIDTH), 0.02),
        "w_s": nrm(ks[19], (L, N_SGU_GROUPS, CHUNK, CHUNK), CHUNK ** -0.5),
        "b_s": 1.0 + nrm(ks[20], (L, N_SGU_GROUPS, CHUNK), 0.01),
        "w_o": nrm(ks[21], (L, MIX_WIDTH, D), MIX_WIDTH ** -0.5),
        "w_ff1": nrm(ks[22], (L, D, D_FF), D ** -0.5),
        "w_ff2": nrm(ks[23], (L, D_FF, D), D_FF ** -0.5),
    }


def reference(x_prompt, x_sample, cache_k, cache_v, c, c_ctx, w_ada, b_ada,
              norm1_w, norm2_w, w_in, q_norm_w, k_norm_w, lambda_q1, lambda_k1,
              lambda_q2, lambda_k2, subln_w, sgu_norm_w, w_s, b_s, w_o,
              w_ff1, w_ff2):
    y_p = x_prompt
    y_s = x_sample
    cos, sin = grid_rope_tables(x_sample.shape[1], x_sample.dtype)
    new_k, new_v = [], []
    for l in range(DEPTH):
        lam_init = 0.8 - 0.6 * math.exp(-0.3 * l)
        lam = (jnp.exp(jnp.sum(lambda_q1[l].astype(jnp.float32) * lambda_k1[l].astype(jnp.float32)))
               - jnp.exp(jnp.sum(lambda_q2[l].astype(jnp.float32) * lambda_k2[l].astype(jnp.float32)))
               + lam_init)

        sh1, sc1, g1, sh2, sc2, g2 = adaln(c_ctx[None, :], w_ada[l], b_ada[l])
        h = modulate(y_p, norm1_w[l], sh1, sc1)
        q, k, v, u, g = project_mixers(h, w_in[l], q_norm_w[l], k_norm_w[l])
        new_k.append(k)
        new_v.append(v)
        att = diff_attention(q, k, v, lam, subln_w[l], lam_init)
        sgu = spatial_gating(u, g, sgu_norm_w[l], w_s[l], b_s[l])
        y_p = y_p + g1 * (jnp.concatenate([att, sgu], axis=-1) @ w_o[l])
        h = modulate(y_p, norm2_w[l], sh2, sc2)
        y_p = y_p + g2 * sq_relu_mlp(h, w_ff1[l], w_ff2[l])

        sh1, sc1, g1, sh2, sc2, g2 = adaln(c, w_ada[l], b_ada[l])
        h = modulate(y_s, norm1_w[l], sh1, sc1)
        q, k, v, u, g = project_mixers(h, w_in[l], q_norm_w[l], k_norm_w[l])
        q = apply_rope(q, cos, sin)
        k = apply_rope(k, cos, sin)
        k_all = jnp.concatenate([cache_k[:, l].astype(k.dtype), k], axis=1)
        v_all = jnp.concatenate([cache_v[:, l].astype(v.dtype), v], axis=1)
        att = diff_attention(q, k_all, v_all, lam, subln_w[l], lam_init)
        sgu = spatial_gating(u, g, sgu_norm_w[l], w_s[l], b_s[l])
        y_s = y_s + g1 * (jnp.concatenate([att, sgu], axis=-1) @ w_o[l])
        h = modulate(y_s, norm2_w[l], sh2, sc2)
        y_s = y_s + g2 * sq_relu_mlp(h, w_ff1[l], w_ff2[l])

    state_k = jnp.stack(new_k, axis=1)
    state_v = jnp.stack(new_v, axis=1)
    return (y_p, y_s, state_k, state_v)
```

```python
import math
from contextlib import ExitStack

import numpy as np
import concourse.bass as bass
import concourse.mybir as mybir
from concourse.bass_utils import run_bass_kernel_spmd

F32 = mybir.dt.float32
BF16 = mybir.dt.bfloat16
AF = mybir.ActivationFunctionType
ALU = mybir.AluOpType

D = 2048
NCORES = 8
EPS = 1e-6
LAM_INIT = 0.8 - 0.6 * math.exp(-0.3 * 0)
NTOK = 1536
NALL = 2560
NKEY = 2816
HALF = 768


class _Stop(Exception):
    pass


KSTOP = [99]
DBG = {}


class _Node:
    __slots__ = ("eng", "signal", "dma")


class Sched:
    def __init__(self, nc, es):
        self.nc = nc
        self.es = es
        self.eng = {"pe": nc.tensor, "act": nc.scalar, "dve": nc.vector, "pool": nc.gpsimd, "sp": nc.sync}
        self.esem = {e: es.enter_context(nc.semaphore("c_" + e)) for e in self.eng}
        self.cnt = {e: 0 for e in self.eng}
        self.dsem = {}
        self.dcnt = {}
        self.waited = {e: {} for e in self.eng}
        self.last_w = {}
        self.readers = {}

    def _wait(self, eng, sem_name, sem, val):
        if self.waited[eng].get(sem_name, 0) >= val:
            return
        self.eng[eng].wait_ge(sem, val)
        self.waited[eng][sem_name] = val

    def add(self, eng, fn, reads=(), writes=(), key=None):
        deps = {}
        for r in reads:
            w = self.last_w.get(r)
            if w is not None:
                deps[id(w)] = w
        for r in writes:
            w = self.last_w.get(r)
            if w is not None:
                deps[id(w)] = w
            for rd in self.readers.get(r, ()):
                deps[id(rd)] = rd
        for d in deps.values():
            if d.eng == "pe" and eng == "pe" and not d.dma and key is None:
                continue
            self._wait(eng, d.signal[0], d.signal[1], d.signal[2])
        ins = fn(self.eng[eng])
        node = _Node()
        node.eng = eng
        node.dma = key is not None
        if key is not None:
            name = "d_" + str(key)
            if name not in self.dsem:
                self.dsem[name] = self.es.enter_context(self.nc.semaphore(name))
                self.dcnt[name] = 0
            self.dcnt[name] += 1
            ins.then_inc(self.dsem[name], 16)
            node.signal = (name, self.dsem[name], 16 * self.dcnt[name])
        else:
            self.cnt[eng] += 1
            ins.then_inc(self.esem[eng], 1)
            node.signal = ("c_" + eng, self.esem[eng], self.cnt[eng])
        for r in reads:
            self.readers.setdefault(r, []).append(node)
        for r in writes:
            self.last_w[r] = node
            self.readers[r] = []
        return node

    def barrier(self, engines=("pe", "act", "dve", "pool", "sp")):
        for e in engines:
            for f in self.eng:
                if self.cnt[f] > 0:
                    self._wait(e, "c_" + f, self.esem[f], self.cnt[f])
            for name, sem in self.dsem.items():
                self._wait(e, name, sem, 16 * self.dcnt[name])
        self.last_w = {}
        self.readers = {}


class Ring:
    def __init__(self, name, tiles):
        self.name = name
        self.tiles = tiles
        self.i = 0

    def next(self):
        j = self.i % len(self.tiles)
        self.i += 1
        return self.tiles[j], (self.name, j)


def build_program():
    nc = bass.Bass("TRN2", target_bir_lowering=False)

    def din(name, shape):
        return nc.dram_tensor(name, list(shape), F32, kind="ExternalInput").ap()

    def dout(name, shape):
        return nc.dram_tensor(name, list(shape), F32, kind="ExternalOutput").ap()

    xp = din("xp", [512, D]); xs = din("xs", [2048, D])
    ck = din("ck", [256, 1024]); cv = din("cv", [256, 1024])
    cT = din("cT", [128, 32]); w_ada = din("w_ada", [D, 6 * D]); bada2 = din("bada2", [2, 6 * D])
    nw1 = din("nw1", [128, 16]); nw2 = din("nw2", [128, 16])
    w_in = din("w_in", [D, 5120]); qkw = din("qkw", [128, 2]); lamv = din("lamv", [128, 256])
    sublnw = din("sublnw", [128, 128]); sgunw = din("sgunw", [128, 1024])
    wsT = din("wsT", [128, 1024]); bsr = din("bsr", [1, 1024])
    w_o = din("w_o", [D, D]); w_ff1 = din("w_ff1", [D, 4 * D]); w_ff2 = din("w_ff2", [4 * D, D])
    c_ident = din("c_ident", [128, 128]); c_bones = din("c_bones", [128, 128]); c_perm = din("c_perm", [128, 128])
    c_sel = din("c_sel", [2, 256]); c_cos = din("c_cos", [128, 2048]); c_sin = din("c_sin", [128, 2048])
    yp = dout("yp", [512, D]); ys = dout("ys", [1024, D]); sk = dout("sk", [512, 1024]); sv = dout("sv", [512, 1024])

    w_ada_v = w_ada.rearrange("(k p) n -> p k n", p=128)
    gsc = nc.dram_tensor("gate_rows", [4, D], F32)
    w_in_v = w_in.rearrange("(k p) n -> p k n", p=128)
    w_o_v = w_o.rearrange("(k p) n -> p k n", p=128)
    w_ff1_v = w_ff1.rearrange("(k p) n -> p k n", p=128)
    w_ff2_v = w_ff2.rearrange("(k p) n -> p k n", p=128)

    def xrows(tt):
        if tt < 4:
            return xp[tt * 128:(tt + 1) * 128, :]
        return xs[(tt - 4) * 128:(tt - 3) * 128, :]

    def yrows(tt):
        if tt < 4:
            return yp[tt * 128:(tt + 1) * 128, :]
        return ys[(tt - 4) * 128:(tt - 3) * 128, :]

    top = ExitStack()
    try:
        _body(nc, top, locals())
    except _Stop:
        pass
    return nc


def _body(nc, top, L):
    globals().update({k: v for k, v in L.items() if k not in ("nc", "top")})
    with top:
        S = Sched(nc, top)
        A = S.add

        def sbt(es, name, shape, dt=F32):
            return es.enter_context(nc.sbuf_tensor(name, list(shape), dt))

        def pst(es, name, shape, dt=F32):
            return es.enter_context(nc.psum_tensor(name, list(shape), dt))

        pg = Ring("pg", [pst(top, "pg%d" % i, [128, 512]) for i in range(2)])
        pS = Ring("pS", [pst(top, "pS%d" % i, [128, 512]) for i in range(2)])
        pO = [pst(top, "pO%d" % i, [128, 512]) for i in range(2)]
        pT = Ring("pT", [pst(top, "pT%d" % i, [128, 1024], BF16) for i in range(2)])
        p6_tiles = [pg.tiles[0], pg.tiles[1], pS.tiles[0], pS.tiles[1], pO[0], pO[1]]
        p6_names = [("pg", 0), ("pg", 1), ("pS", 0), ("pS", 1), ("pO", 0), ("pO", 1)]
        p6_i = [0]

        def p6_next():
            j = p6_i[0] % 6
            p6_i[0] += 1
            return p6_tiles[j], p6_names[j]

        identb = sbt(top, "identb", [128, 128], BF16)
        identf = sbt(top, "identf", [128, 128])
        bones = sbt(top, "bones", [128, 128], BF16)
        perm = sbt(top, "perm", [128, 128], BF16)
        sel = sbt(top, "sel", [2, 256])
        zerob = sbt(top, "zerob", [128, 260], BF16)
        epsc = sbt(top, "epsc", [128, 1])
        A1 = sbt(top, "A1", [128, 16, 2]); S1 = sbt(top, "S1", [128, 16, 2])
        A2 = sbt(top, "A2", [128, 16, 2]); S2 = sbt(top, "S2", [128, 16, 2])
        scT = sbt(top, "scT", [128, 16, 2], BF16)
        nw2_t = sbt(top, "nw2_t", [128, 16])
        attT = sbt(top, "attT", [128, 8, NTOK], BF16)
        shr = sbt(top, "shr", [128, 16384], BF16)

        def sguT_ap(g, c0, n):
            return shr[:, g * NTOK + c0:g * NTOK + c0 + n]
        qkw_t = sbt(top, "qkw_t", [128, 2])
        neglam = sbt(top, "neglam", [128, 1])
        subw = sbt(top, "subw", [128, 128])
        st_a = Ring("st_a", [sbt(top, "st_a%d" % i, [128, 8]) for i in range(4)])
        st_b = Ring("st_b", [sbt(top, "st_b%d" % i, [128, 8]) for i in range(4)])

        A("pool", lambda e: e.dma_start(out=identb[:], in_=c_ident), writes=["identb"], key="identb")
        A("sp", lambda e: e.dma_start(out=identf[:], in_=c_ident), writes=["identf"], key="identf")
        A("pool", lambda e: e.dma_start(out=bones[:], in_=c_bones), writes=["bones"], key="bones")
        A("pool", lambda e: e.dma_start(out=perm[:], in_=c_perm), writes=["perm"], key="perm")
        A("sp", lambda e: e.dma_start(out=sel[:], in_=c_sel), writes=["sel"], key="sel")
        A("sp", lambda e: e.dma_start(out=qkw_t[:], in_=qkw), writes=["qkw"], key="qkw")
        A("sp", lambda e: e.dma_start(out=subw[:], in_=sublnw), writes=["subw"], key="subw")
        A("dve", lambda e: e.memset(zerob[:], 0.0), writes=["zerob"])
        A("dve", lambda e: e.memset(epsc[:], EPS), writes=["epsc"])
        A("dve", lambda e: e.tensor_scalar(out=subw[:], in0=subw[:], scalar1=1.0 - LAM_INIT, scalar2=None, op0=ALU.mult),
          reads=["subw"], writes=["subw"])

        def chk(phase):
            if KSTOP[0] <= phase:
                S.barrier(engines=("sp",))
                raise _Stop()

        with ExitStack() as p0:
            lam_t = sbt(p0, "lam_t", [128, 256]); lam_p = sbt(p0, "lam_p", [128, 128]); lam_s = sbt(p0, "lam_s", [128, 4])
            A("sp", lambda e: e.dma_start(out=lam_t[:], in_=lamv), writes=["lam_t"], key="lam_t")
            for j in range(2):
                A("dve", lambda e, j=j: e.tensor_tensor(out=lam_p[:, j * 64:(j + 1) * 64], in0=lam_t[:, (2 * j) * 64:(2 * j + 1) * 64],
                                                          in1=lam_t[:, (2 * j + 1) * 64:(2 * j + 2) * 64], op=ALU.mult),
                  reads=["lam_t"], writes=[("lam_p", j)])
                A("dve", lambda e, j=j: e.reduce_sum(out=lam_s[:, j:j + 1], in_=lam_p[:, j * 64:(j + 1) * 64], axis=mybir.AxisListType.X),
                  reads=[("lam_p", j)], writes=[("lam_s", j)])
                A("act", lambda e, j=j: e.activation(out=lam_s[:, 2 + j:3 + j], in_=lam_s[:, j:j + 1], func=AF.Exp),
                  reads=[("lam_s", j)], writes=[("lam_e", j)])
            A("dve", lambda e: e.tensor_tensor(out=neglam[:], in0=lam_s[:, 3:4], in1=lam_s[:, 2:3], op=ALU.subtract),
              reads=[("lam_e", 0), ("lam_e", 1)], writes=["neglam"])
            A("dve", lambda e: e.tensor_scalar(out=neglam[:], in0=neglam[:], scalar1=-LAM_INIT, scalar2=None, op0=ALU.add),
              reads=["neglam"], writes=["neglam"])

            cT_t = sbt(p0, "cT_t", [128, 32])
            mod = sbt(p0, "mod", [2, 2 * D])
            nw1_t = sbt(p0, "nw1_t", [128, 16])
            wa = Ring("wa", [sbt(p0, "wa%d" % i, [128, 16, 512], BF16) for i in range(3)])
            A("sp", lambda e: e.dma_start(out=cT_t[:], in_=cT), writes=["cT_t"], key="cT_t")
            A("sp", lambda e: e.dma_start(out=mod[:], in_=bada2[:, 0:2 * D]), writes=[("mod", b) for b in range(8)], key="bada_t")
            A("sp", lambda e: e.dma_start(out=nw1_t[:], in_=nw1), writes=["nw1_t"], key="nw1_t")
            A("sp", lambda e: e.dma_start(out=nw2_t[:], in_=nw2), writes=["nw2_t"], key="nw2_t")
            A("act", lambda e: e.activation(out=scT[:].rearrange("p k r -> p (k r)"), in_=cT_t[:], func=AF.Silu), reads=["cT_t"], writes=["scT"])
            for blk in range(8):
                wt, wr = wa.next()
                A("pool", lambda e, wt=wt, blk=blk: e.dma_start(out=wt[:], in_=w_ada_v[:, :, blk * 512:(blk + 1) * 512]), writes=[wr], key=wr)
                pt, pr = pg.next()

                def mm(e, wt=wt, pt=pt):
                    for kc in range(16):
                        ins = e.matmul(pt[0:2, :], lhsT=scT[:, kc, :], rhs=wt[:, kc, :], start=(kc == 0), stop=(kc == 15))
                    return ins
                A("pe", mm, reads=[wr, "scT"], writes=[pr])
                A("dve", lambda e, pt=pt, blk=blk: e.tensor_tensor(out=mod[0:2, blk * 512:(blk + 1) * 512], in0=pt[0:2, :],
                                                                    in1=mod[0:2, blk * 512:(blk + 1) * 512], op=ALU.add),
                  reads=[pr, ("mod", blk)], writes=[("mod", blk)])
            for ci, (chunk, dst) in enumerate([(0, S1), (1, A1)]):
                pt, pr = pg.next()

                def tr(e, pt=pt, chunk=chunk):
                    for kc in range(16):
                        c0 = chunk * D + kc * 128
                        ins = e.transpose(pt[:, kc * 2:kc * 2 + 2], mod[0:2, c0:c0 + 128], identf[0:2, 0:2])
                    return ins
                A("pe", tr, reads=[("mod", b) for b in range(chunk * 4, chunk * 4 + 4)] + ["identf"], writes=[pr])
                A("dve", lambda e, pt=pt, dst=dst: e.tensor_copy(out=dst[:].rearrange("p k r -> p (k r)"), in_=pt[:, 0:32]),
                  reads=[pr], writes=[("modc", ci)])
            for r in range(2):
                A("dve", lambda e, r=r: e.scalar_tensor_tensor(out=A1[:, :, r], in0=A1[:, :, r], scalar=1.0, in1=nw1_t[:], op0=ALU.add, op1=ALU.mult),
                  reads=[("modc", 1), "nw1_t"], writes=[("modc", 1)])
            S.barrier()
        chk(0)
        MODC = [("modc", i) for i in range(2)]

        def norm_stats(src_ap, src_res, xn_ring):
            sa, sar = st_a.next()
            xn, xnr = xn_ring.next()
            A("act", lambda e: e.activation(out=xn[:], in_=src_ap, func=AF.Square, accum_out=sa[:, 0:1]), reads=[src_res], writes=[xnr, sar])
            A("act", lambda e: e.activation(out=sa[:, 1:2], in_=sa[:, 0:1], func=AF.Sqrt, bias=epsc[:, 0:1], scale=1.0 / D),
              reads=[sar, "epsc"], writes=[sar])
            A("dve", lambda e: e.reciprocal(out=sa[:, 2:3], in_=sa[:, 1:2]), reads=[sar], writes=[sar])
            A("dve", lambda e: e.tensor_scalar(out=xn[:], in0=src_ap, scalar1=sa[:, 2:3], scalar2=None, op0=ALU.mult),
              reads=[src_res, sar], writes=[xnr])
            return xn, xnr

        def norm_tr(xn, xnr, Acol, Scol, r, dst, dst_res, col0):
            for q in range(2):
                pt, pr = pT.next()

                def tr(e, pt=pt, q=q):
                    for j in range(8):
                        kc = q * 8 + j
                        ins = e.transpose(pt[:, j * 128:(j + 1) * 128], xn[:, kc * 128:(kc + 1) * 128], identb[:])
                    return ins
                A("pe", tr, reads=[xnr, "identb"], writes=[pr])
                for j in range(8):
                    kc = q * 8 + j
                    if q == 0:
                        A("act", lambda e, pt=pt, j=j, kc=kc: e.activation(out=dst(kc, col0, 128), in_=pt[:, j * 128:(j + 1) * 128], func=AF.Identity,
                                                                          bias=Scol[:, kc, r:r + 1], scale=Acol[:, kc, r:r + 1]),
                          reads=[pr] + MODC, writes=[(dst_res, kc)])
                    else:
                        A("dve", lambda e, pt=pt, j=j, kc=kc: e.tensor_scalar(out=dst(kc, col0, 128), in0=pt[:, j * 128:(j + 1) * 128],
                                                                             scalar1=Acol[:, kc, r:r + 1], scalar2=Scol[:, kc, r:r + 1],
                                                                             op0=ALU.mult, op1=ALU.add),
                          reads=[pr] + MODC, writes=[(dst_res, kc)])

        with ExitStack() as pf:
            hT_own = sbt(pf, "hT_own", [128, 16, NTOK], BF16)

            def hTs(kc, c0, n):
                if c0 < NTOK:
                    return hT_own[:, kc, c0:c0 + n]
                return shr[:, kc * 1024 + (c0 - NTOK):kc * 1024 + (c0 - NTOK) + n]
            with ExitStack() as pn:
                xt_ring = Ring("xt", [sbt(pn, "xt%d" % i, [128, D]) for i in range(2)])
                xn_ring = Ring("xn", [sbt(pn, "xn%d" % i, [128, D], BF16) for i in range(2)])
                prev = None
                for tt in range(20):
                    xt, xr = xt_ring.next()
                    A("sp", lambda e, xt=xt, tt=tt: e.dma_start(out=xt[:], in_=xrows(tt)), writes=[xr], key=xr)
                    cur = norm_stats(xt[:], xr, xn_ring) + (tt,)
                    if prev is not None:
                        norm_tr(prev[0], prev[1], A1, S1, 0 if prev[2] < 4 else 1, hTs, ("hT", prev[2]), prev[2] * 128)
                    prev = cur
                norm_tr(prev[0], prev[1], A1, S1, 1, hTs, ("hT", prev[2]), prev[2] * 128)
                S.barrier()
            chk(1)

            with ExitStack() as pa:
                cosT = sbt(pa, "cosT", [128, 2048], BF16); sinT = sbt(pa, "sinT", [128, 2048], BF16)
                A("pool", lambda e: e.dma_start(out=cosT[:], in_=c_cos), writes=["cosT"], key="cosT")
                A("pool", lambda e: e.dma_start(out=sinT[:], in_=c_sin), writes=["sinT"], key="sinT")
                wsm = Ring("wsm", [sbt(pa, "wsm%d" % i, [128, 16, 128], BF16) for i in range(3)])
                qT = sbt(pa, "qT", [128, NTOK], BF16)
                kT = [sbt(pa, "kT%d" % c, [128, NKEY], BF16) for c in range(2)]
                vv = sbt(pa, "vv", [128, 22, 130], BF16)
                ckb = sbt(pa, "ckb", [128, 2, 128], BF16); cvb = sbt(pa, "cvb", [128, 2, 128], BF16)
                sq_ring = Ring("sq", [sbt(pa, "sq%d" % i, [128, 512], BF16) for i in range(2)])
                rs_ring = Ring("rs", [sbt(pa, "rs%d" % i, [128, 512]) for i in range(2)])
                zn_ring = Ring("zn", [sbt(pa, "zn%d" % i, [128, 512], BF16) for i in range(2)])
                t1_ring = Ring("t1", [sbt(pa, "t1%d" % i, [128, 512]) for i in range(2)])
                t2_ring = Ring("t2", [sbt(pa, "t2%d" % i, [128, 512]) for i in range(2)])
                zf_ring = Ring("zf", [sbt(pa, "zf%d" % i, [128, 512]) for i in range(1)])
                vb_ring = Ring("vb", [sbt(pa, "vb%d" % i, [128, 512], BF16) for i in range(1)])
                so_ring = Ring("so", [sbt(pa, "so%d" % i, [128, 4, 128]) for i in range(1)])
                PT_ring = Ring("PT", [sbt(pa, "PT%d" % i, [128, 512], BF16) for i in range(4)])
                a0_t = sbt(pa, "a0_t", [128, 4, 128])
                o_raw = [sbt(pa, "o_raw%d" % c, [128, 4, 130]) for c in range(2)]
                o_t = sbt(pa, "o_t", [128, 4, 128])
                on_t = sbt(pa, "on_t", [128, 4, 128], BF16)
                A("dve", lambda e: e.memset(kT[0][64:128, :], 0.0), writes=["kTpad0"])
                A("dve", lambda e: e.memset(kT[1][0:64, :], 0.0), writes=["kTpad1"])
                A("dve", lambda e: e.memset(vv[:, :, 128:130], 1.0), writes=["vones"])
                wa2 = Ring("wa2", [sbt(pa, "wa2%d" % i, [128, 16, 256], BF16) for i in range(2)])
                mch = sbt(pa, "mch", [2, D])

                ada_blocks = [(chunk, blk) for chunk in (3, 4, 2, 5) for blk in range(8)]
                ada_state = {"k": 0, "slots": {}}

                def ada_dma(k):
                    chunk, blk = ada_blocks[k]
                    wt, wr = wa2.next()
                    c0 = chunk * D + blk * 256
                    A("pool", lambda e: e.dma_start(out=wt[:], in_=w_ada_v[:, :, c0:c0 + 256]), writes=[wr], key=wr)
                    ada_state["slots"][k] = (wt, wr)

                def ada_finish(chunk):
                    MCH = [("mch", b) for b in range(8)]
                    if chunk in (3, 4):
                        dst = S2 if chunk == 3 else A2
                        pt, pr = pS.next()

                        def tr(e):
                            for kc in range(16):
                                ins = e.transpose(pt[:, kc * 2:kc * 2 + 2], mch[0:2, kc * 128:(kc + 1) * 128], identf[0:2, 0:2])
                            return ins
                        A("pe", tr, reads=MCH + ["identf"], writes=[pr])
                        A("dve", lambda e: e.tensor_copy(out=dst[:].rearrange("p k r -> p (k r)"), in_=pt[:, 0:32]), reads=[pr], writes=[("modc2", chunk)])
                        if chunk == 4:
                            for r in range(2):
                                A("dve", lambda e, r=r: e.scalar_tensor_tensor(out=A2[:, :, r], in0=A2[:, :, r], scalar=1.0, in1=nw2_t[:], op0=ALU.add, op1=ALU.mult),
                                  reads=[("modc2", 4), "nw2_t"], writes=[("modc2", 4)])
                    else:
                        gi = 0 if chunk == 2 else 1
                        A("sp", lambda e: e.dma_start(out=gsc[2 * gi:2 * gi + 2, :], in_=mch[:]), reads=MCH, key=("gsc", gi))

                def ada_next(n=1):
                    for _ in range(n):
                        k = ada_state["k"]
                        if k >= len(ada_blocks):
                            return
                        ada_state["k"] = k + 1
                        if k == 0:
                            ada_dma(0)
                        if k + 1 < len(ada_blocks):
                            ada_dma(k + 1)
                        chunk, blk = ada_blocks[k]
                        if blk == 0:
                            A("sp", lambda e: e.dma_start(out=mch[:], in_=bada2[:, chunk * D:(chunk + 1) * D]),
                              writes=[("mch", b) for b in range(8)], key="mch")
                        wt, wr = ada_state["slots"].pop(k)
                        pt, pr = pg.tiles[1], ("pg", 1)

                        def mm(e):
                            for kc in range(16):
                                ins = e.matmul(pt[0:2, 0:256], lhsT=scT[:, kc, :], rhs=wt[:, kc, :], start=(kc == 0), stop=(kc == 15))
                            return ins
                        A("pe", mm, reads=[wr, "scT"], writes=[pr])
                        A("dve", lambda e: e.tensor_tensor(out=mch[0:2, blk * 256:(blk + 1) * 256], in0=pt[0:2, 0:256],
                                                           in1=mch[0:2, blk * 256:(blk + 1) * 256], op=ALU.add),
                          reads=[pr, ("mch", blk)], writes=[("mch", blk)])
                        if blk == 7:
                            ada_finish(chunk)

                pq = Ring("pq", [pg.tiles[0], pg.tiles[1], pO[0], pO[1]])
                pq_names = [("pg", 0), ("pg", 1), ("pO", 0), ("pO", 1)]

                def pq_next():
                    j = pq.i % 4
                    pq.i += 1
                    return pq.tiles[j], pq_names[j]

                def proj_block(wt, wr, tb, ring4=False):
                    pt, pr = pq_next() if ring4 else pg.next()

                    def mm(e):
                        for kc in range(16):
                            ins = e.matmul(pt[:], lhsT=wt[:, kc, :], rhs=hTs(kc, tb * 512, 512), start=(kc == 0), stop=(kc == 15))
                        return ins
                    A("pe", mm, reads=[wr] + [("hT", tb * 4 + j) for j in range(4)], writes=[pr])
                    return pt, pr

                deferred = []

                def state_out(zf, zfr, dram, h):
                    pt, pr = pg.tiles[1], ("pg", 1)

                    def tr(e):
                        for j in range(4):
                            ins = e.transpose(pt[:, j * 128:(j + 1) * 128], zf[:, j * 128:(j + 1) * 128], identf[:])
                        return ins
                    A("pe", tr, reads=[zfr, "identf"], writes=[pr])
                    so, sor = so_ring.next()
                    A("act", lambda e: e.activation(out=so[:].rearrange("p a b -> p (a b)"), in_=pt[:], func=AF.Copy), reads=[pr], writes=[sor])
                    A("sp", lambda e: e.dma_start(out=dram[:, h * 128:(h + 1) * 128].rearrange("(t p) n -> p t n", p=128), in_=so[:]),
                      reads=[sor], key=sor)

                def stA(st):
                    st["pt"], st["pr"] = proj_block(st["wt"], st["wr"], st["tb"], ring4=True)
                    st["sq"], st["sqr"] = sq_ring.next()
                    A("act", lambda e: e.activation(out=st["sq"][:], in_=st["pt"][:], func=AF.Square), reads=[st["pr"]], writes=[st["sqr"]])

                def stB(st, h):
                    pt, pr, sq, sqr, tb, kind = st["pt"], st["pr"], st["sq"], st["sqr"], st["tb"], st["kind"]
                    wcol = 0 if kind == "q" else 1
                    c0 = tb * 512
                    p2, p2r = pS.next()
                    A("pe", lambda e: e.matmul(p2[:], lhsT=bones[:], rhs=sq[:], start=True, stop=True), reads=[sqr, "bones"], writes=[p2r])
                    rs, rsr = rs_ring.next()
                    A("act", lambda e: e.activation(out=rs[:], in_=p2[:], func=AF.Sqrt, bias=epsc[:, 0:1], scale=1.0), reads=[p2r, "epsc"], writes=[rsr])
                    A("dve", lambda e: e.reciprocal(out=rs[:], in_=rs[:]), reads=[rsr], writes=[rsr])
                    if tb == 0:
                        zf, zfr = zf_ring.next()
                        A("dve", lambda e: e.scalar_tensor_tensor(out=zf[:], in0=pt[:], scalar=qkw_t[:, wcol:wcol + 1], in1=rs[:], op0=ALU.mult, op1=ALU.mult),
                          reads=[pr, rsr, "qkw"], writes=[zfr])
                        if kind == "q":
                            A("act", lambda e: e.activation(out=qT[:, c0:c0 + 512], in_=zf[:], func=AF.Copy), reads=[zfr], writes=[("qT", tb)])
                        else:
                            A("dve", lambda e: e.tensor_copy(out=kT[0][0:64, c0:c0 + 512], in_=zf[0:64, :]), reads=[zfr], writes=[("kT0", tb)])
                            A("act", lambda e: e.activation(out=kT[1][64:128, c0:c0 + 512], in_=zf[64:128, :], func=AF.Copy), reads=[zfr], writes=[("kT1", tb)])
                        st["zf"], st["zfr"] = zf, zfr
                    else:
                        zn, znr = zn_ring.next()
                        A("dve", lambda e: e.scalar_tensor_tensor(out=zn[:], in0=pt[:], scalar=qkw_t[:, wcol:wcol + 1], in1=rs[:], op0=ALU.mult, op1=ALU.mult),
                          reads=[pr, rsr, "qkw"], writes=[znr])
                        st["zn"], st["znr"] = zn, znr

                def stC(st, h):
                    tb, kind = st["tb"], st["kind"]
                    c0 = tb * 512
                    if tb == 0:
                        if kind == "k":
                            deferred.append(lambda zf=st["zf"], zfr=st["zfr"], h=h: state_out(zf, zfr, sk, h))
                        return
                    zn, znr = st["zn"], st["znr"]
                    p3, p3r = pS.next()
                    A("pe", lambda e: e.matmul(p3[:], lhsT=perm[:], rhs=zn[:], start=True, stop=True), reads=[znr, "perm"], writes=[p3r])
                    r0 = (tb - 1) * 512
                    t1, t1r = t1_ring.next()
                    A("pool", lambda e: e.tensor_tensor(out=t1[:], in0=zn[:], in1=cosT[:, r0:r0 + 512], op=ALU.mult), reads=[znr, "cosT"], writes=[t1r])
                    t2, t2r = t2_ring.next()
                    A("dve", lambda e: e.tensor_tensor(out=t2[:], in0=p3[:], in1=sinT[:, r0:r0 + 512], op=ALU.mult), reads=[p3r, "sinT"], writes=[t2r])
                    if kind == "q":
                        A("pool", lambda e: e.tensor_tensor(out=qT[:, c0:c0 + 512], in0=t1[:], in1=t2[:], op=ALU.add), reads=[t1r, t2r], writes=[("qT", tb)])
                    else:
                        A("pool", lambda e: e.tensor_tensor(out=kT[0][0:64, c0:c0 + 512], in0=t1[0:64, :], in1=t2[0:64, :], op=ALU.add),
                          reads=[t1r, t2r], writes=[("kT0", tb)])
                        A("dve", lambda e: e.tensor_tensor(out=kT[1][64:128, c0:c0 + 512], in0=t1[64:128, :], in1=t2[64:128, :], op=ALU.add),
                          reads=[t1r, t2r], writes=[("kT1", tb)])

                s4_tiles = [pS.tiles[0], pS.tiles[1], pg.tiles[0]]
                s4_names = [("pS", 0), ("pS", 1), ("pg", 0)]
                s4_i = [0]

                def s4_next():
                    j = s4_i[0] % 3
                    s4_i[0] += 1
                    return s4_tiles[j], s4_names[j]

                def attention(h, q0, nq, groups, prev):
                    nqt = nq // 128
                    nbk = (nqt + 1) // 2
                    pO_res = [("pO", 0), ("pO", 1)]
                    single = len(groups) == 1
                    maxlen = max(len(g[2]) for g in groups)
                    for c in range(2):
                        for b in range(nbk):
                            A("pe", lambda e, b=b: e.matmul(pO[b][:, 0:260], lhsT=zerob[:, 0:128], rhs=zerob[:, 0:260], start=True, stop=True,
                                                           skip_group_check=True),
                              reads=["zerob"], writes=[pO_res[b]])

                        def s_mm(g, kt, c=c):
                            qa = q0 + g[0] * 128
                            nqg = g[1] * 128
                            ps_, psr = s4_next()
                            A("pe", lambda e: e.matmul(ps_[:, 0:nqg], lhsT=kT[c][:, kt * 128:(kt + 1) * 128], rhs=qT[:, qa:qa + nqg],
                                                       start=True, stop=True),
                              reads=[("kT%d" % c, kt // 4), "kTpad%d" % c] + [("qT", q0 // 512)], writes=[psr])
                            return ps_, psr
                        nxt = []
                        if single:
                            for d_ in range(min(2, len(groups[0][2]))):
                                nxt.append(s_mm(groups[0], groups[0][2][d_]))
                        for ki in range(maxlen):
                            cur = {}
                            for gi, g in enumerate(groups):
                                if ki >= len(g[2]):
                                    continue
                                if single:
                                    cur[gi] = nxt.pop(0)
                                    if ki + 2 < len(g[2]):
                                        nxt.append(s_mm(g, g[2][ki + 2]))
                                else:
                                    cur[gi] = s_mm(g, g[2][ki])
                            for gi, g in enumerate(groups):
                                if gi not in cur:
                                    continue
                                ps_, psr = cur[gi]
                                nqg = g[1] * 128
                                kt = g[2][ki]
                                P, Pr = PT_ring.next()
                                A("act", lambda e, ps_=ps_, P=P, nqg=nqg: e.activation(out=P[:, 0:nqg], in_=ps_[:, 0:nqg], func=AF.Exp, scale=0.125), reads=[psr], writes=[Pr])

                                def av(e, P=P, kt=kt, g=g):
                                    for j in range(g[1]):
                                        qi = g[0] + j
                                        bank = pO[qi // 2]; off = (qi % 2) * 130
                                        ins = e.matmul(bank[:, off:off + 129], lhsT=P[:, j * 128:(j + 1) * 128], rhs=vv[:, kt, 0:129],
                                                       start=False, stop=(ki == len(g[2]) - 1), skip_group_check=True)
                                    return ins
                                A("pe", av, reads=[Pr, ("vv", kt), "vones"], writes=sorted(set(pO_res[(g[0] + j) // 2] for j in range(g[1]))))
                            if ki % 9 == 6:
                                ada_next(1)
                            if ki == 3 and deferred:
                                deferred.pop(0)()
                            if c == 0 and prev is not None and ki == min(3, maxlen - 1):
                                prev[0]()
                        if c == 0 and prev is not None:
                            prev[1]()
                        for b in range(nbk):
                            A("dve", lambda e, b=b, c=c: e.tensor_copy(out=o_raw[c][:, 2 * b:2 * b + 2, :].rearrange("p a b -> p (a b)"), in_=pO[b][:, 0:260]),
                              reads=[pO_res[b]], writes=[("oraw", c, b)])
                    ORAW = [("oraw", c, b) for c in range(2) for b in range(nbk)]
                    sb_, sbr = st_b.next()
                    for c in range(2):
                        A("dve", lambda e, c=c: e.reciprocal(out=sb_[:, 4 * c:4 * c + nqt], in_=o_raw[c][:, 0:nqt, 128]), reads=ORAW, writes=[(sbr, c)])
                    A("dve", lambda e: e.tensor_scalar(out=sb_[:, 4:4 + nqt], in0=sb_[:, 4:4 + nqt], scalar1=neglam[:, 0:1], scalar2=None, op0=ALU.mult),
                      reads=[(sbr, 1), "neglam"], writes=[(sbr, 1)])
                    for qi in range(nqt):
                        A("dve", lambda e, qi=qi: e.tensor_scalar(out=a0_t[:, qi, :], in0=o_raw[0][:, qi, 0:128], scalar1=sb_[:, qi:qi + 1], scalar2=None, op0=ALU.mult),
                          reads=ORAW + [(sbr, 0)], writes=[("a0", qi)])
                        A("dve", lambda e, qi=qi: e.scalar_tensor_tensor(out=o_t[:, qi, :], in0=o_raw[1][:, qi, 0:128], scalar=sb_[:, 4 + qi:5 + qi], in1=a0_t[:, qi, :],
                                                                        op0=ALU.mult, op1=ALU.add),
                          reads=ORAW + [(sbr, 1), ("a0", qi)], writes=[("o_t", qi)])
                    OT = [("o_t", qi) for qi in range(nqt)]
                    A("dve", lambda e: e.tensor_tensor(out=a0_t[:, 0:nqt, :], in0=o_t[:, 0:nqt, :], in1=o_t[:, 0:nqt, :], op=ALU.mult),
                      reads=OT, writes=[("a0", qi) for qi in range(nqt)])
                    sc_, scr = st_a.next()
                    A("dve", lambda e: e.reduce_sum(out=sc_[:, 0:nqt], in_=a0_t[:, 0:nqt, :], axis=mybir.AxisListType.X),
                      reads=[("a0", qi) for qi in range(nqt)], writes=[scr])

                    def fin2():
                        A("act", lambda e: e.activation(out=sc_[:, 4:4 + nqt], in_=sc_[:, 0:nqt], func=AF.Ln, bias=epsc[:, 0:1], scale=1.0 / 128), reads=[scr, "epsc"], writes=[scr])
                        A("act", lambda e: e.activation(out=sc_[:, 4:4 + nqt], in_=sc_[:, 4:4 + nqt], func=AF.Exp, scale=-0.5), reads=[scr], writes=[scr])
                        for qi in range(nqt):
                            A("dve", lambda e, qi=qi: e.scalar_tensor_tensor(out=on_t[:, qi, :], in0=o_t[:, qi, :], scalar=sc_[:, 4 + qi:5 + qi], in1=subw[:], op0=ALU.mult, op1=ALU.mult),
                              reads=[("o_t", qi), scr, "subw"], writes=[("on_t", qi)])

                    def tail():
                        pt, pr = pT.next()

                        def tr(e):
                            for qi in range(nqt):
                                ins = e.transpose(pt[:, qi * 128:(qi + 1) * 128], on_t[:, qi, :], identb[:])
                            return ins
                        A("pe", tr, reads=[("on_t", qi) for qi in range(nqt)] + ["identb"], writes=[pr])
                        A("act", lambda e: e.activation(out=attT[:, h, q0:q0 + nq], in_=pt[:, 0:nq], func=AF.Copy),
                          reads=[pr], writes=[("attT", h, q0)])
                    return fin2, tail

                def v_block(h, wv, wvr, tb):
                    pt, pr = proj_block(wv, wvr, tb, ring4=True)
                    vb, vbr = vb_ring.next()
                    A("act", lambda e: e.activation(out=vb[:], in_=pt[:], func=AF.Copy), reads=[pr], writes=[vbr])
                    if tb == 0:
                        p4, p4r = pS.next()

                        def mmv(e):
                            for j in range(4):
                                for kc in range(16):
                                    ins = e.matmul(p4[:, j * 128:(j + 1) * 128], lhsT=hTs(kc, j * 128, 128), rhs=wv[:, kc, :], start=(kc == 0), stop=(kc == 15))
                            return ins
                        A("pe", mmv, reads=[wvr] + [("hT", j) for j in range(4)], writes=[p4r])
                        so, sor = so_ring.next()
                        A("act", lambda e: e.activation(out=so[:].rearrange("p a b -> p (a b)"), in_=p4[:], func=AF.Copy), reads=[p4r], writes=[sor])
                        A("sp", lambda e: e.dma_start(out=sv[:, h * 128:(h + 1) * 128].rearrange("(t p) n -> p t n", p=128), in_=so[:]),
                          reads=[sor], key=sor)
                    ptb, ptbr = pT.next()

                    def trv(e):
                        for j in range(4):
                            ins = e.transpose(ptb[:, j * 128:(j + 1) * 128], vb[:, j * 128:(j + 1) * 128], identb[:])
                        return ins
                    A("pe", trv, reads=[vbr, "identb"], writes=[ptbr])
                    for j in range(4):
                        A("dve", lambda e, j=j: e.tensor_copy(out=vv[:, tb * 4 + j, 0:128], in_=ptb[:, j * 128:(j + 1) * 128]),
                          reads=[ptbr], writes=[("vv", tb * 4 + j)])

                prev = None
                for h in range(8):
                    wq, wqr = wsm.next()
                    A("pool", lambda e, wq=wq, h=h: e.dma_start(out=wq[:], in_=w_in_v[:, :, h * 128:(h + 1) * 128]), writes=[wqr], key=wqr)
                    wk, wkr = wsm.next()
                    A("pool", lambda e, wk=wk, h=h: e.dma_start(out=wk[:], in_=w_in_v[:, :, 1024 + h * 128:1024 + (h + 1) * 128]), writes=[wkr], key=wkr)
                    wv, wvr = wsm.next()
                    A("pool", lambda e, wv=wv, h=h: e.dma_start(out=wv[:], in_=w_in_v[:, :, 2048 + h * 128:2048 + (h + 1) * 128]), writes=[wvr], key=wvr)
                    A("pool", lambda e, h=h: e.dma_start(out=ckb[:], in_=ck[:, h * 128:(h + 1) * 128].rearrange("(t p) n -> p t n", p=128)), writes=["ckb"], key="ckb")
                    A("pool", lambda e, h=h: e.dma_start(out=cvb[:], in_=cv[:, h * 128:(h + 1) * 128].rearrange("(t p) n -> p t n", p=128)), writes=["cvb"], key="cvb")
                    blocks = ([dict(kind="k", tb=tb, wt=wk, wr=wkr) for tb in (1, 2, 3, 4)] + [dict(kind="q", tb=tb, wt=wq, wr=wqr) for tb in (1, 2)]
                              + [dict(kind="q", tb=0, wt=wq, wr=wqr), dict(kind="k", tb=0, wt=wk, wr=wkr)])
                    vorder = [1, 2, 3, 4, 0]
                    nb_ = len(blocks)
                    vdone = 0
                    for step in range(nb_ + 2):
                        if step < nb_:
                            stA(blocks[step])
                        if step == 1 and prev is not None:
                            prev[0]()
                            prev[1]()
                            prev = None
                        if 0 <= step - 1 < nb_:
                            stB(blocks[step - 1], h)
                        if 0 <= step - 2 < nb_:
                            stC(blocks[step - 2], h)
                        if step >= 2 and vdone < 5:
                            v_block(h, wv, wvr, vorder[vdone])
                            vdone += 1
                    while vdone < 5:
                        v_block(h, wv, wvr, vorder[vdone])
                        vdone += 1
                    ptb, ptbr = pT.next()

                    def trc(e, ptb=ptb):
                        for j in range(2):
                            ins = e.transpose(ptb[:, j * 128:(j + 1) * 128], ckb[:, j, :], identb[:])
                        return ins
                    A("pe", trc, reads=["ckb", "identb"], writes=[ptbr])
                    A("dve", lambda e, ptb=ptb: e.tensor_copy(out=kT[0][0:64, NALL:NKEY], in_=ptb[0:64, 0:256]), reads=[ptbr], writes=[("kT0", 5)])
                    A("dve", lambda e, ptb=ptb: e.tensor_copy(out=kT[1][64:128, NALL:NKEY], in_=ptb[64:128, 0:256]), reads=[ptbr], writes=[("kT1", 5)])
                    for j in range(2):
                        A("dve", lambda e, j=j: e.tensor_copy(out=vv[:, 20 + j, 0:128], in_=cvb[:, j, :]), reads=["cvb"], writes=[("vv", 20 + j)])
                    prev = attention(h, 512, 512, [(0, 4, list(range(4, 22)))], prev)
                    prev = attention(h, 1024, 512, [(0, 4, list(range(4, 22)))], prev)
                    prev = attention(h, 0, 512, [(0, 2, [0, 1]), (2, 2, [2, 3])], prev)
                    if h == 0:
                        chk(2)
                prev[0]()
                prev[1]()
                ada_next(100)
                S.barrier()
            chk(3)

            with ExitStack() as pg_:
                wg = sbt(pg_, "wg", [128, 16, 1024], BF16)
                gn = sbt(pg_, "gn", [128, 12, 1024], BF16)
                sgw = sbt(pg_, "sgw", [128, 1024]); wsb = sbt(pg_, "wsb", [128, 1024], BF16); bsf = sbt(pg_, "bsf", [1, 1024]); onesf = sbt(pg_, "onesf", [1, 128])
                gg_ring = Ring("gg", [sbt(pg_, "gg%d" % i, [128, 1024]) for i in range(1)])
                gu_ring = Ring("gu", [sbt(pg_, "gu%d" % i, [128, 512], BF16) for i in range(2)])
                wu_ring = Ring("wu", [sbt(pg_, "wu%d" % i, [128, 16, 128], BF16) for i in range(2)])
                for nq_ in range(4):
                    A("pool", lambda e, nq_=nq_: e.dma_start(out=wg[:, :, nq_ * 256:(nq_ + 1) * 256], in_=w_in_v[:, :, 4096 + nq_ * 256:4096 + (nq_ + 1) * 256]),
                      writes=[("wg", nq_)], key=("wg", nq_))
                A("sp", lambda e: e.dma_start(out=sgw[:], in_=sgunw), writes=["sgw"], key="sgw")
                A("pool", lambda e: e.dma_start(out=wsb[:], in_=wsT), writes=["wsb"], key="wsb")
                A("sp", lambda e: e.dma_start(out=bsf[:], in_=bsr), writes=["bsf"], key="bsf")
                A("dve", lambda e: e.memset(onesf[:], 1.0), writes=["onesf"])
                for tt in range(12):
                    gg, ggr = gg_ring.next()
                    sa, sar = st_a.next()
                    for nb in range(2):
                        pt, pr = p6_next()

                        def mm(e, pt=pt, nb=nb, tt=tt):
                            for kc in range(16):
                                ins = e.matmul(pt[:], lhsT=hTs(kc, tt * 128, 128), rhs=wg[:, kc, nb * 512:(nb + 1) * 512], start=(kc == 0), stop=(kc == 15))
                            return ins
                        A("pe", mm, reads=[("wg", 2 * nb), ("wg", 2 * nb + 1), ("hT", tt)], writes=[pr])
                        A("act", lambda e, pt=pt, nb=nb, gg=gg, sa=sa: e.activation(out=gg[:, nb * 512:(nb + 1) * 512], in_=pt[:], func=AF.Gelu_apprx_tanh,
                                                                                   accum_out=sa[:, nb:nb + 1]),
                          reads=[pr], writes=[(ggr, nb), (sar, nb)])
                    gjr = ("gn", tt)
                    A("act", lambda e, gg=gg, sa=sa, tt=tt: e.activation(out=gn[:, tt, :], in_=gg[:], func=AF.Square, accum_out=sa[:, 2:3]),
                      reads=[(ggr, 0), (ggr, 1)], writes=[gjr, (sar, 2)])
                    A("dve", lambda e, sa=sa: e.tensor_tensor(out=sa[:, 3:4], in0=sa[:, 0:1], in1=sa[:, 1:2], op=ALU.add), reads=[(sar, 0), (sar, 1)], writes=[(sar, 3)])
                    A("dve", lambda e, sa=sa: e.tensor_scalar(out=sa[:, 3:4], in0=sa[:, 3:4], scalar1=1.0 / 1024, scalar2=None, op0=ALU.mult), reads=[(sar, 3)], writes=[(sar, 3)])
                    A("dve", lambda e, sa=sa: e.tensor_tensor(out=sa[:, 4:5], in0=sa[:, 3:4], in1=sa[:, 3:4], op=ALU.mult), reads=[(sar, 3)], writes=[(sar, 4)])
                    A("dve", lambda e, sa=sa: e.scalar_tensor_tensor(out=sa[:, 5:6], in0=sa[:, 2:3], scalar=1.0 / 1024, in1=sa[:, 4:5], op0=ALU.mult, op1=ALU.subtract),
                      reads=[(sar, 2), (sar, 4)], writes=[(sar, 5)])
                    A("act", lambda e, sa=sa: e.activation(out=sa[:, 6:7], in_=sa[:, 5:6], func=AF.Sqrt, bias=epsc[:, 0:1], scale=1.0), reads=[(sar, 5), "epsc"], writes=[(sar, 6)])
                    A("dve", lambda e, sa=sa: e.reciprocal(out=sa[:, 6:7], in_=sa[:, 6:7]), reads=[(sar, 6)], writes=[(sar, 6)])
                    A("dve", lambda e, sa=sa: e.scalar_tensor_tensor(out=sa[:, 7:8], in0=sa[:, 3:4], scalar=-1.0, in1=sa[:, 6:7], op0=ALU.mult, op1=ALU.mult),
                      reads=[(sar, 3), (sar, 6)], writes=[(sar, 7)])
                    A("act", lambda e, gg=gg, sa=sa: e.activation(out=gg[:], in_=gg[:], func=AF.Identity, bias=sa[:, 7:8], scale=sa[:, 6:7]),
                      reads=[(ggr, 0), (ggr, 1), (sar, 6), (sar, 7), gjr], writes=[(ggr, 0), (ggr, 1)])
                    A("dve", lambda e, gg=gg, tt=tt: e.tensor_tensor(out=gn[:, tt, :], in0=gg[:], in1=sgw[:], op=ALU.mult), reads=[(ggr, 0), (ggr, 1), "sgw"], writes=[("gn", tt)])
                for gi in range(8):
                    wu, wur = wu_ring.next()
                    A("pool", lambda e, wu=wu, gi=gi: e.dma_start(out=wu[:], in_=w_in_v[:, :, 3072 + gi * 128:3072 + (gi + 1) * 128]), writes=[wur], key=wur)
                    for tb in range(3):
                        pt, pr = p6_next()

                        def mm(e, pt=pt, wu=wu, tb=tb):
                            for kc in range(16):
                                ins = e.matmul(pt[:], lhsT=wu[:, kc, :], rhs=hTs(kc, tb * 512, 512), start=(kc == 0), stop=(kc == 15))
                            return ins
                        A("pe", mm, reads=[wur] + [("hT", tb * 4 + j) for j in range(4)], writes=[pr])
                        gu, gur = gu_ring.next()
                        A("act", lambda e, pt=pt, gu=gu: e.activation(out=gu[:], in_=pt[:], func=AF.Gelu_apprx_tanh), reads=[pr], writes=[gur])
                        pm, pmr = p6_next()

                        def mix(e, pm=pm, gi=gi, tb=tb):
                            for j in range(4):
                                tt = tb * 4 + j
                                e.matmul(pm[:, j * 128:(j + 1) * 128], lhsT=gn[:, tt, gi * 128:(gi + 1) * 128], rhs=wsb[:, gi * 128:(gi + 1) * 128], start=True, stop=False)
                                ins = e.matmul(pm[:, j * 128:(j + 1) * 128], lhsT=onesf[0:1, :], rhs=bsf[0:1, gi * 128:(gi + 1) * 128], start=False, stop=True)
                            return ins
                        A("pe", mix, reads=[("gn", tb * 4 + j) for j in range(4)] + ["wsb", "bsf", "onesf"], writes=[pmr])
                        A("dve", lambda e, pm=pm, gu=gu, gi=gi, tb=tb: e.tensor_tensor(out=sguT_ap(gi, tb * 512, 512), in0=pm[:], in1=gu[:], op=ALU.mult),
                          reads=[pmr, gur], writes=[("sguT", gi, tb)])
                S.barrier()
            chk(4)

        with ExitStack() as pb:
            acc = sbt(pb, "acc", [128, 6, D])
            h2T = sbt(pb, "h2T", [128, 16, HALF], BF16)
            aT = sbt(pb, "aT", [128, 16, HALF], BF16)
            wb = Ring("wb", [sbt(pb, "wb%d" % i, [128, 16, 256], BF16) for i in range(2)])
            xst_ring = Ring("xst", [sbt(pb, "xst%d" % i, [128, 256]) for i in range(3)])
            tmp_ring = Ring("tmp", [sbt(pb, "tmp%d" % i, [128, 256]) for i in range(3)])
            rl_ring = Ring("rl", [sbt(pb, "rl%d" % i, [128, 512], BF16) for i in range(2)])
            xn2_ring = Ring("xn2", [sbt(pb, "xn2%d" % i, [128, D], BF16) for i in range(2)])
            GATES = [("gates", gi, r, nb) for gi in range(2) for r in range(2) for nb in range(4)]
            gates = sbt(pb, "gates", [128, 4, D], BF16)
            for gi in range(2):
                A("sp", lambda e, gi=gi: e.dma_start(out=acc[0:2, gi, :], in_=gsc[2 * gi:2 * gi + 2, :]), writes=[("gstage", gi)], key=("gstage", gi))
                for r in range(2):
                    for nb in range(4):
                        pt, pr = p6_next()
                        A("pe", lambda e, pt=pt, r=r, gi=gi, nb=nb: e.matmul(pt[:], lhsT=sel[0:2, r * 128:(r + 1) * 128], rhs=acc[0:2, gi, nb * 512:(nb + 1) * 512],
                                                                          start=True, stop=True),
                          reads=["sel", ("gstage", gi)], writes=[pr])
                        A("act", lambda e, pt=pt, gi=gi, r=r, nb=nb: e.activation(out=gates[:, gi * 2 + r, nb * 512:(nb + 1) * 512], in_=pt[:], func=AF.Copy),
                          reads=[pr], writes=[("gates", gi, r, nb)])
            S.barrier()
            GATES = []
            for half in range(2):
                t0 = half * 6
                for nb in range(8):
                    wt, wr = wb.next()
                    A("pool", lambda e, wt=wt, nb=nb: e.dma_start(out=wt[:], in_=w_o_v[:, :, nb * 256:(nb + 1) * 256]), writes=[wr], key=wr)
                    for ti in range(6):
                        tt = t0 + ti
                        r = 0 if tt < 4 else 1
                        pt, pr = p6_next()

                        def mm(e, pt=pt, wt=wt, tt=tt):
                            for kc in range(16):
                                lt = attT[:, kc, tt * 128:(tt + 1) * 128] if kc < 8 else sguT_ap(kc - 8, tt * 128, 128)
                                ins = e.matmul(pt[:, 0:256], lhsT=lt, rhs=wt[:, kc, :], start=(kc == 0), stop=(kc == 15))
                            return ins
                        A("pe", mm, reads=[wr] + [("sguT", g_, tt // 4) for g_ in range(8)], writes=[pr])
                        xst, xsr = xst_ring.next()
                        A("sp", lambda e, xst=xst, tt=tt, nb=nb: e.dma_start(out=xst[:], in_=xrows(tt)[:, nb * 256:(nb + 1) * 256]), writes=[xsr], key=xsr)
                        tmp, tmr = tmp_ring.next()
                        A("dve", lambda e, pt=pt, tmp=tmp, r=r, nb=nb: e.tensor_tensor(out=tmp[:], in0=pt[:, 0:256], in1=gates[:, r, nb * 256:(nb + 1) * 256], op=ALU.mult),
                          reads=[pr] + GATES, writes=[tmr])
                        A("dve", lambda e, tmp=tmp, xst=xst, ti=ti, nb=nb: e.tensor_tensor(out=acc[:, ti, nb * 256:(nb + 1) * 256], in0=tmp[:], in1=xst[:], op=ALU.add),
                          reads=[tmr, xsr], writes=[("acc", ti)])
                h2d = (lambda kc, c0, n: h2T[:, kc, c0:c0 + n])
                prev = None
                for ti in range(6):
                    cur = norm_stats(acc[:, ti, :], ("acc", ti), xn2_ring) + (ti,)
                    if prev is not None:
                        norm_tr(prev[0], prev[1], A2, S2, 0 if t0 + prev[2] < 4 else 1, h2d, ("h2T", prev[2]), prev[2] * 128)
                    prev = cur
                norm_tr(prev[0], prev[1], A2, S2, 0 if t0 + prev[2] < 4 else 1, h2d, ("h2T", prev[2]), prev[2] * 128)
                for rd in range(4):
                    for fb in range(8):
                        wt, wr = wb.next()
                        f0 = rd * 2048 + fb * 256
                        A("pool", lambda e, wt=wt, f0=f0: e.dma_start(out=wt[:], in_=w_ff1_v[:, :, f0:f0 + 256]), writes=[wr], key=wr)
                        for fj in range(2):
                            fc = fb * 2 + fj
                            for (c0, cn) in ((0, 512), (512, 256)):
                                pt, pr = p6_next()

                                def mm(e, pt=pt, wt=wt, fj=fj, c0=c0, cn=cn):
                                    for kc in range(16):
                                        ins = e.matmul(pt[:, 0:cn], lhsT=wt[:, kc, fj * 128:(fj + 1) * 128], rhs=h2T[:, kc, c0:c0 + cn], start=(kc == 0), stop=(kc == 15))
                                    return ins
                                A("pe", mm, reads=[wr] + [(("h2T", ti), kc) for ti in range(c0 // 128, (c0 + cn) // 128) for kc in range(16)], writes=[pr])
                                rl, rlr = rl_ring.next()
                                A("act", lambda e, pt=pt, rl=rl, cn=cn: e.activation(out=rl[:, 0:cn], in_=pt[:, 0:cn], func=AF.Relu), reads=[pr], writes=[rlr])
                                A("dve", lambda e, rl=rl, fc=fc, c0=c0, cn=cn: e.tensor_tensor(out=aT[:, fc, c0:c0 + cn], in0=rl[:, 0:cn], in1=rl[:, 0:cn], op=ALU.mult),
                                  reads=[rlr], writes=[("aT", fc, c0)])
                    for nb in range(8):
                        wt, wr = wb.next()
                        A("pool", lambda e, wt=wt, rd=rd, nb=nb: e.dma_start(out=wt[:], in_=w_ff2_v[:, rd * 16:(rd + 1) * 16, nb * 256:(nb + 1) * 256]), writes=[wr], key=wr)
                        for ti in range(6):
                            tt = t0 + ti
                            r = 0 if tt < 4 else 1
                            pt, pr = p6_next()

                            def mm(e, pt=pt, wt=wt, ti=ti):
                                for fc in range(16):
                                    ins = e.matmul(pt[:, 0:256], lhsT=aT[:, fc, ti * 128:(ti + 1) * 128], rhs=wt[:, fc, :], start=(fc == 0), stop=(fc == 15))
                                return ins
                            A("pe", mm, reads=[wr] + [("aT", fc, 0 if ti < 4 else 512) for fc in range(16)], writes=[pr])
                            tmp, tmr = tmp_ring.next()
                            A("dve", lambda e, pt=pt, tmp=tmp, r=r, nb=nb: e.tensor_tensor(out=tmp[:], in0=pt[:, 0:256], in1=gates[:, 2 + r, nb * 256:(nb + 1) * 256], op=ALU.mult),
                              reads=[pr] + GATES, writes=[tmr])
                            A("dve", lambda e, tmp=tmp, ti=ti, nb=nb: e.tensor_tensor(out=acc[:, ti, nb * 256:(nb + 1) * 256], in0=acc[:, ti, nb * 256:(nb + 1) * 256], in1=tmp[:], op=ALU.add),
                              reads=[tmr, ("acc", ti)], writes=[("acc", ti)])
                for ti in range(6):
                    tt = t0 + ti
                    A("sp", lambda e, ti=ti, tt=tt: e.dma_start(out=yrows(tt), in_=acc[:, ti, :]), reads=[("acc", ti)], key=("yout", ti))
                chk(5 + half)
            S.barrier(engines=("sp",))
    return nc


def _rope_tables(order):
    n_freq = 16
    freqs = (10000.0 ** (-np.arange(n_freq, dtype=np.float32) / n_freq)).astype(np.float32)
    r = (order // 64).astype(np.float32)
    col = (order % 64).astype(np.float32)
    ang_r = r[:, None] * freqs
    ang_c = col[:, None] * freqs
    ang = np.concatenate([ang_r, ang_r, ang_c, ang_c], axis=-1)
    cos = np.cos(ang).astype(np.float32).T
    sin = np.sin(ang).astype(np.float32).T
    return np.ascontiguousarray(np.concatenate([cos, cos], 0)), np.ascontiguousarray(np.concatenate([sin, sin], 0))


def _consts():
    ident = np.eye(128, dtype=np.float32)
    bones = np.zeros((128, 128), np.float32)
    bones[0:64, 0:64] = 1.0 / 64
    bones[64:128, 64:128] = 1.0 / 64
    perm = np.zeros((128, 128), np.float32)
    for d in range(128):
        if d % 32 < 16:
            perm[d + 16, d] = -1.0
        else:
            perm[d - 16, d] = 1.0
    sel = np.zeros((2, 256), np.float32)
    sel[0, 0:128] = 1.0
    sel[1, 128:256] = 1.0
    return ident, bones, perm, sel


_NC_CACHE = {}


def kernel(x_prompt, x_sample, cache_k, cache_v, c, c_ctx, w_ada, b_ada, norm1_w, norm2_w, w_in,
           q_norm_w, k_norm_w, lambda_q1, lambda_k1, lambda_q2, lambda_k2, subln_w, sgu_norm_w,
           w_s, b_s, w_o, w_ff1, w_ff2):
    f = lambda a: np.ascontiguousarray(np.asarray(a, dtype=np.float32))
    x_prompt, x_sample, cache_k, cache_v, c, c_ctx = map(f, (x_prompt, x_sample, cache_k, cache_v, c, c_ctx))
    ident, bones, perm, sel = _consts()
    col = lambda v: np.ascontiguousarray(f(v).reshape(16, 128).T)
    shared = {
        "w_ada": f(w_ada)[0], "bada2": np.ascontiguousarray(np.stack([f(b_ada)[0]] * 2)),
        "nw1": col(norm1_w[0]), "nw2": col(norm2_w[0]), "w_in": f(w_in)[0],
        "qkw": np.ascontiguousarray(np.stack([np.tile(f(q_norm_w)[0], 2), np.tile(f(k_norm_w)[0], 2)], axis=1)),
        "lamv": np.ascontiguousarray(np.broadcast_to(np.concatenate([f(lambda_q1)[0], f(lambda_k1)[0], f(lambda_q2)[0], f(lambda_k2)[0]])[None, :], (128, 256))),
        "sublnw": np.ascontiguousarray(np.broadcast_to(f(subln_w)[0][None, :], (128, 128))),
        "sgunw": np.ascontiguousarray(np.broadcast_to(f(sgu_norm_w)[0][None, :], (128, 1024))),
        "wsT": np.ascontiguousarray(f(w_s)[0].transpose(2, 0, 1).reshape(128, 1024)),
        "bsr": np.ascontiguousarray(f(b_s)[0].reshape(1, 1024)),
        "w_o": f(w_o)[0], "w_ff1": f(w_ff1)[0], "w_ff2": f(w_ff2)[0],
        "c_ident": ident, "c_bones": bones, "c_perm": perm, "c_sel": sel,
    }
    in_maps = []
    for i in range(NCORES):
        b, s = i // 2, i % 2
        own = np.arange(s * 1024, (s + 1) * 1024)
        oth = np.arange((1 - s) * 1024, (2 - s) * 1024)
        order = np.concatenate([own, oth])
        cos, sin = _rope_tables(order)
        cvec = np.stack([c_ctx, c[b]], axis=0)
        cT = np.ascontiguousarray(cvec.reshape(2, 16, 128).transpose(2, 1, 0).reshape(128, 32))
        m = dict(shared)
        m.update({
            "xp": np.ascontiguousarray(x_prompt[2 * i:2 * i + 2].reshape(512, D)),
            "xs": np.ascontiguousarray(x_sample[b][order]),
            "ck": np.ascontiguousarray(cache_k[b, 0].reshape(256, 1024)),
            "cv": np.ascontiguousarray(cache_v[b, 0].reshape(256, 1024)),
            "cT": cT, "c_cos": cos, "c_sin": sin,
        })
        in_maps.append(m)
    if "nc" not in _NC_CACHE:
        _NC_CACHE["nc"] = build_program()
    res = run_bass_kernel_spmd(_NC_CACHE["nc"], in_maps, core_ids=list(range(NCORES)))
    outs = res.results
    y_p = np.zeros((16, 256, D), np.float32)
    y_s = np.zeros((4, 2048, D), np.float32)
    s_k = np.zeros((16, 1, 256, 8, 2, 64), np.float32)
    s_v = np.zeros((16, 1, 256, 8, 128), np.float32)
    for i in range(NCORES):
        b, s = i // 2, i % 2
        y_p[2 * i:2 * i + 2] = outs[i]["yp"].reshape(2, 256, D)
        y_s[b, s * 1024:(s + 1) * 1024] = outs[i]["ys"]
        s_k[2 * i:2 * i + 2, 0] = outs[i]["sk"].reshape(2, 256, 8, 2, 64)
        s_v[2 * i:2 * i + 2, 0] = outs[i]["sv"].reshape(2, 256, 8, 128)
    return (y_p, y_s, s_k, s_v)
```

```python
import math
from contextlib import ExitStack

import numpy as np
import concourse.bass as bass
import concourse.mybir as mybir
from concourse.bass_utils import run_bass_kernel_spmd

F32 = mybir.dt.float32
BF16 = mybir.dt.bfloat16
AF = mybir.ActivationFunctionType
ALU = mybir.AluOpType

D = 2048
NCORES = 8
EPS = 1e-6
LAM_INIT = 0.8 - 0.6 * math.exp(-0.3 * 0)
NTOK = 1536
NALL = 2560
NKEY = 2816
HALF = 768


class _Stop(Exception):
    pass


KSTOP = [99]
DBG = {}


class _Node:
    __slots__ = ("eng", "signal", "dma")


class Sched:
    def __init__(self, nc, es):
        self.nc = nc
        self.es = es
        self.eng = {"pe": nc.tensor, "act": nc.scalar, "dve": nc.vector, "pool": nc.gpsimd, "sp": nc.sync}
        self.esem = {e: es.enter_context(nc.semaphore("c_" + e)) for e in self.eng}
        self.cnt = {e: 0 for e in self.eng}
        self.dsem = {}
        self.dcnt = {}
        self.waited = {e: {} for e in self.eng}
        self.last_w = {}
        self.readers = {}

    def _wait(self, eng, sem_name, sem, val):
        if self.waited[eng].get(sem_name, 0) >= val:
            return
        self.eng[eng].wait_ge(sem, val)
        self.waited[eng][sem_name] = val

    def add(self, eng, fn, reads=(), writes=(), key=None):
        deps = {}
        for r in reads:
            w = self.last_w.get(r)
            if w is not None:
                deps[id(w)] = w
        for r in writes:
            w = self.last_w.get(r)
            if w is not None:
                deps[id(w)] = w
            for rd in self.readers.get(r, ()):
                deps[id(rd)] = rd
        for d in deps.values():
            if d.eng == "pe" and eng == "pe" and not d.dma and key is None:
                continue
            self._wait(eng, d.signal[0], d.signal[1], d.signal[2])
        ins = fn(self.eng[eng])
        node = _Node()
        node.eng = eng
        node.dma = key is not None
        if key is not None:
            name = "d_" + str(key)
            if name not in self.dsem:
                self.dsem[name] = self.es.enter_context(self.nc.semaphore(name))
                self.dcnt[name] = 0
            self.dcnt[name] += 1
            ins.then_inc(self.dsem[name], 16)
            node.signal = (name, self.dsem[name], 16 * self.dcnt[name])
        else:
            self.cnt[eng] += 1
            ins.then_inc(self.esem[eng], 1)
            node.signal = ("c_" + eng, self.esem[eng], self.cnt[eng])
        for r in reads:
            self.readers.setdefault(r, []).append(node)
        for r in writes:
            self.last_w[r] = node
            self.readers[r] = []
        return node

    def barrier(self, engines=("pe", "act", "dve", "pool", "sp")):
        for e in engines:
            for f in self.eng:
                if self.cnt[f] > 0:
                    self._wait(e, "c_" + f, self.esem[f], self.cnt[f])
            for name, sem in self.dsem.items():
                self._wait(e, name, sem, 16 * self.dcnt[name])
        self.last_w = {}
        self.readers = {}


class Ring:
    def __init__(self, name, tiles):
        self.name = name
        self.tiles = tiles
        self.i = 0

    def next(self):
        j = self.i % len(self.tiles)
        self.i += 1
        return self.tiles[j], (self.name, j)


def build_program():
    nc = bass.Bass("TRN2", target_bir_lowering=False)

    def din(name, shape):
        return nc.dram_tensor(name, list(shape), F32, kind="ExternalInput").ap()

    def dout(name, shape):
        return nc.dram_tensor(name, list(shape), F32, kind="ExternalOutput").ap()

    xp = din("xp", [512, D]); xs = din("xs", [2048, D])
    ck = din("ck", [256, 1024]); cv = din("cv", [256, 1024])
    cT = din("cT", [128, 32]); w_ada = din("w_ada", [D, 6 * D]); bada2 = din("bada2", [2, 6 * D])
    nw1 = din("nw1", [128, 16]); nw2 = din("nw2", [128, 16])
    w_in = din("w_in", [D, 5120]); qkw = din("qkw", [128, 2]); lamv = din("lamv", [128, 256])
    sublnw = din("sublnw", [128, 128]); sgunw = din("sgunw", [128, 1024])
    wsT = din("wsT", [128, 1024]); bsr = din("bsr", [1, 1024])
    w_o = din("w_o", [D, D]); w_ff1 = din("w_ff1", [D, 4 * D]); w_ff2 = din("w_ff2", [4 * D, D])
    c_ident = din("c_ident", [128, 128]); c_bones = din("c_bones", [128, 128]); c_perm = din("c_perm", [128, 128])
    c_sel = din("c_sel", [2, 256]); c_cos = din("c_cos", [128, 2048]); c_sin = din("c_sin", [128, 2048])
    yp = dout("yp", [512, D]); ys = dout("ys", [1024, D]); sk = dout("sk", [512, 1024]); sv = dout("sv", [512, 1024])

    w_ada_v = w_ada.rearrange("(k p) n -> p k n", p=128)
    gsc = nc.dram_tensor("gate_rows", [4, D], F32)
    w_in_v = w_in.rearrange("(k p) n -> p k n", p=128)
    w_o_v = w_o.rearrange("(k p) n -> p k n", p=128)
    w_ff1_v = w_ff1.rearrange("(k p) n -> p k n", p=128)
    w_ff2_v = w_ff2.rearrange("(k p) n -> p k n", p=128)

    def xrows(tt):
        if tt < 4:
            return xp[tt * 128:(tt + 1) * 128, :]
        return xs[(tt - 4) * 128:(tt - 3) * 128, :]

    def yrows(tt):
        if tt < 4:
            return yp[tt * 128:(tt + 1) * 128, :]
        return ys[(tt - 4) * 128:(tt - 3) * 128, :]

    top = ExitStack()
    try:
        _body(nc, top, locals())
    except _Stop:
        pass
    return nc


def _body(nc, top, L):
    globals().update({k: v for k, v in L.items() if k not in ("nc", "top")})
    with top:
        S = Sched(nc, top)
        A = S.add

        def sbt(es, name, shape, dt=F32):
            return es.enter_context(nc.sbuf_tensor(name, list(shape), dt))

        def pst(es, name, shape, dt=F32):
            return es.enter_context(nc.psum_tensor(name, list(shape), dt))

        pg = Ring("pg", [pst(top, "pg%d" % i, [128, 512]) for i in range(2)])
        pS = Ring("pS", [pst(top, "pS%d" % i, [128, 512]) for i in range(2)])
        pO = [pst(top, "pO%d" % i, [128, 512]) for i in range(2)]
        pT = Ring("pT", [pst(top, "pT%d" % i, [128, 1024], BF16) for i in range(2)])
        p6_tiles = [pg.tiles[0], pg.tiles[1], pS.tiles[0], pS.tiles[1], pO[0], pO[1]]
        p6_names = [("pg", 0), ("pg", 1), ("pS", 0), ("pS", 1), ("pO", 0), ("pO", 1)]
        p6_i = [0]

        def p6_next():
            j = p6_i[0] % 6
            p6_i[0] += 1
            return p6_tiles[j], p6_names[j]

        identb = sbt(top, "identb", [128, 128], BF16)
        identf = sbt(top, "identf", [128, 128])
        bones = sbt(top, "bones", [128, 128], BF16)
        perm = sbt(top, "perm", [128, 128], BF16)
        sel = sbt(top, "sel", [2, 256])
        zerob = sbt(top, "zerob", [128, 260], BF16)
        epsc = sbt(top, "epsc", [128, 1])
        A1 = sbt(top, "A1", [128, 16, 2]); S1 = sbt(top, "S1", [128, 16, 2])
        A2 = sbt(top, "A2", [128, 16, 2]); S2 = sbt(top, "S2", [128, 16, 2])
        scT = sbt(top, "scT", [128, 16, 2], BF16)
        nw2_t = sbt(top, "nw2_t", [128, 16])
        attT = sbt(top, "attT", [128, 8, NTOK], BF16)
        shr = sbt(top, "shr", [128, 16384], BF16)

        def sguT_ap(g, c0, n):
            return shr[:, g * NTOK + c0:g * NTOK + c0 + n]
        qkw_t = sbt(top, "qkw_t", [128, 2])
        neglam = sbt(top, "neglam", [128, 1])
        subw = sbt(top, "subw", [128, 128])
        st_a = Ring("st_a", [sbt(top, "st_a%d" % i, [128, 8]) for i in range(4)])
        st_b = Ring("st_b", [sbt(top, "st_b%d" % i, [128, 8]) for i in range(4)])

        A("pool", lambda e: e.dma_start(out=identb[:], in_=c_ident), writes=["identb"], key="identb")
        A("sp", lambda e: e.dma_start(out=identf[:], in_=c_ident), writes=["identf"], key="identf")
        A("pool", lambda e: e.dma_start(out=bones[:], in_=c_bones), writes=["bones"], key="bones")
        A("pool", lambda e: e.dma_start(out=perm[:], in_=c_perm), writes=["perm"], key="perm")
        A("sp", lambda e: e.dma_start(out=sel[:], in_=c_sel), writes=["sel"], key="sel")
        A("sp", lambda e: e.dma_start(out=qkw_t[:], in_=qkw), writes=["qkw"], key="qkw")
        A("sp", lambda e: e.dma_start(out=subw[:], in_=sublnw), writes=["subw"], key="subw")
        A("dve", lambda e: e.memset(zerob[:], 0.0), writes=["zerob"])
        A("dve", lambda e: e.memset(epsc[:], EPS), writes=["epsc"])
        A("dve", lambda e: e.tensor_scalar(out=subw[:], in0=subw[:], scalar1=1.0 - LAM_INIT, scalar2=None, op0=ALU.mult),
          reads=["subw"], writes=["subw"])

        def chk(phase):
            if KSTOP[0] <= phase:
                S.barrier(engines=("sp",))
                raise _Stop()

        with ExitStack() as p0:
            lam_t = sbt(p0, "lam_t", [128, 256]); lam_p = sbt(p0, "lam_p", [128, 128]); lam_s = sbt(p0, "lam_s", [128, 4])
            A("sp", lambda e: e.dma_start(out=lam_t[:], in_=lamv), writes=["lam_t"], key="lam_t")
            for j in range(2):
                A("dve", lambda e, j=j: e.tensor_tensor(out=lam_p[:, j * 64:(j + 1) * 64], in0=lam_t[:, (2 * j) * 64:(2 * j + 1) * 64],
                                                          in1=lam_t[:, (2 * j + 1) * 64:(2 * j + 2) * 64], op=ALU.mult),
                  reads=["lam_t"], writes=[("lam_p", j)])
                A("dve", lambda e, j=j: e.reduce_sum(out=lam_s[:, j:j + 1], in_=lam_p[:, j * 64:(j + 1) * 64], axis=mybir.AxisListType.X),
                  reads=[("lam_p", j)], writes=[("lam_s", j)])
                A("act", lambda e, j=j: e.activation(out=lam_s[:, 2 + j:3 + j], in_=lam_s[:, j:j + 1], func=AF.Exp),
                  reads=[("lam_s", j)], writes=[("lam_e", j)])
            A("dve", lambda e: e.tensor_tensor(out=neglam[:], in0=lam_s[:, 3:4], in1=lam_s[:, 2:3], op=ALU.subtract),
              reads=[("lam_e", 0), ("lam_e", 1)], writes=["neglam"])
            A("dve", lambda e: e.tensor_scalar(out=neglam[:], in0=neglam[:], scalar1=-LAM_INIT, scalar2=None, op0=ALU.add),
              reads=["neglam"], writes=["neglam"])

            cT_t = sbt(p0, "cT_t", [128, 32])
            mod = sbt(p0, "mod", [2, 2 * D])
            nw1_t = sbt(p0, "nw1_t", [128, 16])
            wa = Ring("wa", [sbt(p0, "wa%d" % i, [128, 16, 512], BF16) for i in range(3)])
            A("sp", lambda e: e.dma_start(out=cT_t[:], in_=cT), writes=["cT_t"], key="cT_t")
            A("sp", lambda e: e.dma_start(out=mod[:], in_=bada2[:, 0:2 * D]), writes=[("mod", b) for b in range(8)], key="bada_t")
            A("sp", lambda e: e.dma_start(out=nw1_t[:], in_=nw1), writes=["nw1_t"], key="nw1_t")
            A("sp", lambda e: e.dma_start(out=nw2_t[:], in_=nw2), writes=["nw2_t"], key="nw2_t")
            A("act", lambda e: e.activation(out=scT[:].rearrange("p k r -> p (k r)"), in_=cT_t[:], func=AF.Silu), reads=["cT_t"], writes=["scT"])
            for blk in range(8):
                wt, wr = wa.next()
                A("pool", lambda e, wt=wt, blk=blk: e.dma_start(out=wt[:], in_=w_ada_v[:, :, blk * 512:(blk + 1) * 512]), writes=[wr], key=wr)
                pt, pr = pg.next()

                def mm(e, wt=wt, pt=pt):
                    for kc in range(16):
                        ins = e.matmul(pt[0:2, :], lhsT=scT[:, kc, :], rhs=wt[:, kc, :], start=(kc == 0), stop=(kc == 15))
                    return ins
                A("pe", mm, reads=[wr, "scT"], writes=[pr])
                A("dve", lambda e, pt=pt, blk=blk: e.tensor_tensor(out=mod[0:2, blk * 512:(blk + 1) * 512], in0=pt[0:2, :],
                                                                    in1=mod[0:2, blk * 512:(blk + 1) * 512], op=ALU.add),
                  reads=[pr, ("mod", blk)], writes=[("mod", blk)])
            for ci, (chunk, dst) in enumerate([(0, S1), (1, A1)]):
                pt, pr = pg.next()

                def tr(e, pt=pt, chunk=chunk):
                    for kc in range(16):
                        c0 = chunk * D + kc * 128
                        ins = e.transpose(pt[:, kc * 2:kc * 2 + 2], mod[0:2, c0:c0 + 128], identf[0:2, 0:2])
                    return ins
                A("pe", tr, reads=[("mod", b) for b in range(chunk * 4, chunk * 4 + 4)] + ["identf"], writes=[pr])
                A("dve", lambda e, pt=pt, dst=dst: e.tensor_copy(out=dst[:].rearrange("p k r -> p (k r)"), in_=pt[:, 0:32]),
                  reads=[pr], writes=[("modc", ci)])
            for r in range(2):
                A("dve", lambda e, r=r: e.scalar_tensor_tensor(out=A1[:, :, r], in0=A1[:, :, r], scalar=1.0, in1=nw1_t[:], op0=ALU.add, op1=ALU.mult),
                  reads=[("modc", 1), "nw1_t"], writes=[("modc", 1)])
            S.barrier()
        chk(0)
        MODC = [("modc", i) for i in range(2)]

        def norm_stats(src_ap, src_res, xn_ring):
            sa, sar = st_a.next()
            xn, xnr = xn_ring.next()
            A("act", lambda e: e.activation(out=xn[:], in_=src_ap, func=AF.Square, accum_out=sa[:, 0:1]), reads=[src_res], writes=[xnr, sar])
            A("act", lambda e: e.activation(out=sa[:, 1:2], in_=sa[:, 0:1], func=AF.Sqrt, bias=epsc[:, 0:1], scale=1.0 / D),
              reads=[sar, "epsc"], writes=[sar])
            A("dve", lambda e: e.reciprocal(out=sa[:, 2:3], in_=sa[:, 1:2]), reads=[sar], writes=[sar])
            A("dve", lambda e: e.tensor_scalar(out=xn[:], in0=src_ap, scalar1=sa[:, 2:3], scalar2=None, op0=ALU.mult),
              reads=[src_res, sar], writes=[xnr])
            return xn, xnr

        def norm_tr(xn, xnr, Acol, Scol, r, dst, dst_res, col0):
            for q in range(2):
                pt, pr = pT.next()

                def tr(e, pt=pt, q=q):
                    for j in range(8):
                        kc = q * 8 + j
                        ins = e.transpose(pt[:, j * 128:(j + 1) * 128], xn[:, kc * 128:(kc + 1) * 128], identb[:])
                    return ins
                A("pe", tr, reads=[xnr, "identb"], writes=[pr])
                for j in range(8):
                    kc = q * 8 + j
                    if q == 0:
                        A("act", lambda e, pt=pt, j=j, kc=kc: e.activation(out=dst(kc, col0, 128), in_=pt[:, j * 128:(j + 1) * 128], func=AF.Identity,
                                                                          bias=Scol[:, kc, r:r + 1], scale=Acol[:, kc, r:r + 1]),
                          reads=[pr] + MODC, writes=[(dst_res, kc)])
                    else:
                        A("dve", lambda e, pt=pt, j=j, kc=kc: e.tensor_scalar(out=dst(kc, col0, 128), in0=pt[:, j * 128:(j + 1) * 128],
                                                                             scalar1=Acol[:, kc, r:r + 1], scalar2=Scol[:, kc, r:r + 1],
                                                                             op0=ALU.mult, op1=ALU.add),
                          reads=[pr] + MODC, writes=[(dst_res, kc)])

        with ExitStack() as pf:
            hT_own = sbt(pf, "hT_own", [128, 16, NTOK], BF16)

            def hTs(kc, c0, n):
                if c0 < NTOK:
                    return hT_own[:, kc, c0:c0 + n]
                return shr[:, kc * 1024 + (c0 - NTOK):kc * 1024 + (c0 - NTOK) + n]
            with ExitStack() as pn:
                xt_ring = Ring("xt", [sbt(pn, "xt%d" % i, [128, D]) for i in range(2)])
                xn_ring = Ring("xn", [sbt(pn, "xn%d" % i, [128, D], BF16) for i in range(2)])
                prev = None
                for tt in range(20):
                    xt, xr = xt_ring.next()
                    A("sp", lambda e, xt=xt, tt=tt: e.dma_start(out=xt[:], in_=xrows(tt)), writes=[xr], key=xr)
                    cur = norm_stats(xt[:], xr, xn_ring) + (tt,)
                    if prev is not None:
                        norm_tr(prev[0], prev[1], A1, S1, 0 if prev[2] < 4 else 1, hTs, ("hT", prev[2]), prev[2] * 128)
                    prev = cur
                norm_tr(prev[0], prev[1], A1, S1, 1, hTs, ("hT", prev[2]), prev[2] * 128)
                S.barrier()
            chk(1)

            with ExitStack() as pa:
                cosT = sbt(pa, "cosT", [128, 2048], BF16); sinT = sbt(pa, "sinT", [128, 2048], BF16)
                A("pool", lambda e: e.dma_start(out=cosT[:], in_=c_cos), writes=["cosT"], key="cosT")
                A("pool", lambda e: e.dma_start(out=sinT[:], in_=c_sin), writes=["sinT"], key="sinT")
                wsm = Ring("wsm", [sbt(pa, "wsm%d" % i, [128, 16, 128], BF16) for i in range(3)])
                qT = sbt(pa, "qT", [128, NTOK], BF16)
                kT = [sbt(pa, "kT%d" % c, [128, NKEY], BF16) for c in range(2)]
                vv = sbt(pa, "vv", [128, 22, 130], BF16)
                ckb = sbt(pa, "ckb", [128, 2, 128], BF16); cvb = sbt(pa, "cvb", [128, 2, 128], BF16)
                sq_ring = Ring("sq", [sbt(pa, "sq%d" % i, [128, 512], BF16) for i in range(2)])
                rs_ring = Ring("rs", [sbt(pa, "rs%d" % i, [128, 512]) for i in range(2)])
                zn_ring = Ring("zn", [sbt(pa, "zn%d" % i, [128, 512], BF16) for i in range(2)])
                t1_ring = Ring("t1", [sbt(pa, "t1%d" % i, [128, 512]) for i in range(2)])
                t2_ring = Ring("t2", [sbt(pa, "t2%d" % i, [128, 512]) for i in range(2)])
                zf_ring = Ring("zf", [sbt(pa, "zf%d" % i, [128, 512]) for i in range(1)])
                vb_ring = Ring("vb", [sbt(pa, "vb%d" % i, [128, 512], BF16) for i in range(1)])
                so_ring = Ring("so", [sbt(pa, "so%d" % i, [128, 4, 128]) for i in range(1)])
                PT_ring = Ring("PT", [sbt(pa, "PT%d" % i, [128, 512], BF16) for i in range(4)])
                a0_t = sbt(pa, "a0_t", [128, 4, 128])
                o_raw = [sbt(pa, "o_raw%d" % c, [128, 4, 130]) for c in range(2)]
                o_t = sbt(pa, "o_t", [128, 4, 128])
                on_t = sbt(pa, "on_t", [128, 4, 128], BF16)
                A("dve", lambda e: e.memset(kT[0][64:128, :], 0.0), writes=["kTpad0"])
                A("dve", lambda e: e.memset(kT[1][0:64, :], 0.0), writes=["kTpad1"])
                A("dve", lambda e: e.memset(vv[:, :, 128:130], 1.0), writes=["vones"])
                wa2 = Ring("wa2", [sbt(pa, "wa2%d" % i, [128, 16, 256], BF16) for i in range(2)])
                mch = sbt(pa, "mch", [2, D])

                ada_blocks = [(chunk, blk) for chunk in (3, 4, 2, 5) for blk in range(8)]
                ada_state = {"k": 0, "slots": {}}

                def ada_dma(k):
                    chunk, blk = ada_blocks[k]
                    wt, wr = wa2.next()
                    c0 = chunk * D + blk * 256
                    A("pool", lambda e: e.dma_start(out=wt[:], in_=w_ada_v[:, :, c0:c0 + 256]), writes=[wr], key=wr)
                    ada_state["slots"][k] = (wt, wr)

                def ada_finish(chunk):
                    MCH = [("mch", b) for b in range(8)]
                    if chunk in (3, 4):
                        dst = S2 if chunk == 3 else A2
                        pt, pr = pS.next()

                        def tr(e):
                            for kc in range(16):
                                ins = e.transpose(pt[:, kc * 2:kc * 2 + 2], mch[0:2, kc * 128:(kc + 1) * 128], identf[0:2, 0:2])
                            return ins
                        A("pe", tr, reads=MCH + ["identf"], writes=[pr])
                        A("dve", lambda e: e.tensor_copy(out=dst[:].rearrange("p k r -> p (k r)"), in_=pt[:, 0:32]), reads=[pr], writes=[("modc2", chunk)])
                        if chunk == 4:
                            for r in range(2):
                                A("dve", lambda e, r=r: e.scalar_tensor_tensor(out=A2[:, :, r], in0=A2[:, :, r], scalar=1.0, in1=nw2_t[:], op0=ALU.add, op1=ALU.mult),
                                  reads=[("modc2", 4), "nw2_t"], writes=[("modc2", 4)])
                    else:
                        gi = 0 if chunk == 2 else 1
                        A("sp", lambda e: e.dma_start(out=gsc[2 * gi:2 * gi + 2, :], in_=mch[:]), reads=MCH, key=("gsc", gi))

                def ada_next(n=1):
                    for _ in range(n):
                        k = ada_state["k"]
                        if k >= len(ada_blocks):
                            return
                        ada_state["k"] = k + 1
                        if k == 0:
                            ada_dma(0)
                        if k + 1 < len(ada_blocks):
                            ada_dma(k + 1)
                        chunk, blk = ada_blocks[k]
                        if blk == 0:
                            A("sp", lambda e: e.dma_start(out=mch[:], in_=bada2[:, chunk * D:(chunk + 1) * D]),
                              writes=[("mch", b) for b in range(8)], key="mch")
                        wt, wr = ada_state["slots"].pop(k)
                        pt, pr = pg.tiles[1], ("pg", 1)

                        def mm(e):
                            for kc in range(16):
                                ins = e.matmul(pt[0:2, 0:256], lhsT=scT[:, kc, :], rhs=wt[:, kc, :], start=(kc == 0), stop=(kc == 15))
                            return ins
                        A("pe", mm, reads=[wr, "scT"], writes=[pr])
                        A("dve", lambda e: e.tensor_tensor(out=mch[0:2, blk * 256:(blk + 1) * 256], in0=pt[0:2, 0:256],
                                                           in1=mch[0:2, blk * 256:(blk + 1) * 256], op=ALU.add),
                          reads=[pr, ("mch", blk)], writes=[("mch", blk)])
                        if blk == 7:
                            ada_finish(chunk)

                pq = Ring("pq", [pg.tiles[0], pg.tiles[1], pO[0], pO[1]])
                pq_names = [("pg", 0), ("pg", 1), ("pO", 0), ("pO", 1)]

                def pq_next():
                    j = pq.i % 4
                    pq.i += 1
                    return pq.tiles[j], pq_names[j]

                def proj_block(wt, wr, tb, ring4=False):
                    pt, pr = pq_next() if ring4 else pg.next()

                    def mm(e):
                        for kc in range(16):
                            ins = e.matmul(pt[:], lhsT=wt[:, kc, :], rhs=hTs(kc, tb * 512, 512), start=(kc == 0), stop=(kc == 15))
                        return ins
                    A("pe", mm, reads=[wr] + [("hT", tb * 4 + j) for j in range(4)], writes=[pr])
                    return pt, pr

                def state_out(zf, zfr, dram, h):
                    pt, pr = pS.next()

                    def tr(e):
                        for j in range(4):
                            ins = e.transpose(pt[:, j * 128:(j + 1) * 128], zf[:, j * 128:(j + 1) * 128], identf[:])
                        return ins
                    A("pe", tr, reads=[zfr, "identf"], writes=[pr])
                    so, sor = so_ring.next()
                    A("act", lambda e: e.activation(out=so[:].rearrange("p a b -> p (a b)"), in_=pt[:], func=AF.Copy), reads=[pr], writes=[sor])
                    A("sp", lambda e: e.dma_start(out=dram[:, h * 128:(h + 1) * 128].rearrange("(t p) n -> p t n", p=128), in_=so[:]),
                      reads=[sor], key=sor)

                def stA(st):
                    st["pt"], st["pr"] = proj_block(st["wt"], st["wr"], st["tb"], ring4=True)
                    st["sq"], st["sqr"] = sq_ring.next()
                    A("act", lambda e: e.activation(out=st["sq"][:], in_=st["pt"][:], func=AF.Square), reads=[st["pr"]], writes=[st["sqr"]])

                def stB(st, h):
                    pt, pr, sq, sqr, tb, kind = st["pt"], st["pr"], st["sq"], st["sqr"], st["tb"], st["kind"]
                    wcol = 0 if kind == "q" else 1
                    c0 = tb * 512
                    p2, p2r = pS.next()
                    A("pe", lambda e: e.matmul(p2[:], lhsT=bones[:], rhs=sq[:], start=True, stop=True), reads=[sqr, "bones"], writes=[p2r])
                    rs, rsr = rs_ring.next()
                    A("act", lambda e: e.activation(out=rs[:], in_=p2[:], func=AF.Sqrt, bias=epsc[:, 0:1], scale=1.0), reads=[p2r, "epsc"], writes=[rsr])
                    A("dve", lambda e: e.reciprocal(out=rs[:], in_=rs[:]), reads=[rsr], writes=[rsr])
                    if tb == 0:
                        zf, zfr = zf_ring.next()
                        A("dve", lambda e: e.scalar_tensor_tensor(out=zf[:], in0=pt[:], scalar=qkw_t[:, wcol:wcol + 1], in1=rs[:], op0=ALU.mult, op1=ALU.mult),
                          reads=[pr, rsr, "qkw"], writes=[zfr])
                        if kind == "q":
                            A("act", lambda e: e.activation(out=qT[:, c0:c0 + 512], in_=zf[:], func=AF.Copy), reads=[zfr], writes=[("qT", tb)])
                        else:
                            A("dve", lambda e: e.tensor_copy(out=kT[0][0:64, c0:c0 + 512], in_=zf[0:64, :]), reads=[zfr], writes=[("kT0", tb)])
                            A("act", lambda e: e.activation(out=kT[1][64:128, c0:c0 + 512], in_=zf[64:128, :], func=AF.Copy), reads=[zfr], writes=[("kT1", tb)])
                        st["zf"], st["zfr"] = zf, zfr
                    else:
                        zn, znr = zn_ring.next()
                        A("dve", lambda e: e.scalar_tensor_tensor(out=zn[:], in0=pt[:], scalar=qkw_t[:, wcol:wcol + 1], in1=rs[:], op0=ALU.mult, op1=ALU.mult),
                          reads=[pr, rsr, "qkw"], writes=[znr])
                        st["zn"], st["znr"] = zn, znr

                def stC(st, h):
                    tb, kind = st["tb"], st["kind"]
                    c0 = tb * 512
                    if tb == 0:
                        if kind == "k":
                            state_out(st["zf"], st["zfr"], sk, h)
                        return
                    zn, znr = st["zn"], st["znr"]
                    p3, p3r = pS.next()
                    A("pe", lambda e: e.matmul(p3[:], lhsT=perm[:], rhs=zn[:], start=True, stop=True), reads=[znr, "perm"], writes=[p3r])
                    r0 = (tb - 1) * 512
                    t1, t1r = t1_ring.next()
                    A("pool", lambda e: e.tensor_tensor(out=t1[:], in0=zn[:], in1=cosT[:, r0:r0 + 512], op=ALU.mult), reads=[znr, "cosT"], writes=[t1r])
                    t2, t2r = t2_ring.next()
                    A("dve", lambda e: e.tensor_tensor(out=t2[:], in0=p3[:], in1=sinT[:, r0:r0 + 512], op=ALU.mult), reads=[p3r, "sinT"], writes=[t2r])
                    if kind == "q":
                        A("pool", lambda e: e.tensor_tensor(out=qT[:, c0:c0 + 512], in0=t1[:], in1=t2[:], op=ALU.add), reads=[t1r, t2r], writes=[("qT", tb)])
                    else:
                        A("pool", lambda e: e.tensor_tensor(out=kT[0][0:64, c0:c0 + 512], in0=t1[0:64, :], in1=t2[0:64, :], op=ALU.add),
                          reads=[t1r, t2r], writes=[("kT0", tb)])
                        A("dve", lambda e: e.tensor_tensor(out=kT[1][64:128, c0:c0 + 512], in0=t1[64:128, :], in1=t2[64:128, :], op=ALU.add),
                          reads=[t1r, t2r], writes=[("kT1", tb)])

                s4_tiles = [pS.tiles[0], pS.tiles[1], pg.tiles[0]]
                s4_names = [("pS", 0), ("pS", 1), ("pg", 0)]
                s4_i = [0]

                def s4_next():
                    j = s4_i[0] % 3
                    s4_i[0] += 1
                    return s4_tiles[j], s4_names[j]

                def attention(h, q0, nq, groups, prev):
                    nqt = nq // 128
                    nbk = (nqt + 1) // 2
                    pO_res = [("pO", 0), ("pO", 1)]
                    single = len(groups) == 1
                    maxlen = max(len(g[2]) for g in groups)
                    for c in range(2):
                        for b in range(nbk):
                            A("pe", lambda e, b=b: e.matmul(pO[b][:, 0:260], lhsT=zerob[:, 0:128], rhs=zerob[:, 0:260], start=True, stop=True,
                                                           skip_group_check=True),
                              reads=["zerob"], writes=[pO_res[b]])

                        def s_mm(ki, c=c):
                            ps_, psr = s4_next()
                            act = [g for g in groups if ki < len(g[2])]

                            def fn(e):
                                for g in act:
                                    kt = g[2][ki]
                                    a_ = g[0] * 128
                                    n_ = g[1] * 128
                                    ins = e.matmul(ps_[:, a_:a_ + n_], lhsT=kT[c][:, kt * 128:(kt + 1) * 128], rhs=qT[:, q0 + a_:q0 + a_ + n_],
                                                   start=True, stop=True)
                                return ins
                            A("pe", fn, reads=[("kT%d" % c, g[2][ki] // 4) for g in act] + ["kTpad%d" % c, ("qT", q0 // 512)], writes=[psr])
                            return ps_, psr
                        nxt = [s_mm(d_) for d_ in range(min(2, maxlen))]
                        for ki in range(maxlen):
                            ps_, psr = nxt.pop(0)
                            if ki + 2 < maxlen:
                                nxt.append(s_mm(ki + 2))
                            P, Pr = PT_ring.next()
                            A("act", lambda e, ps_=ps_, P=P: e.activation(out=P[:, 0:nq], in_=ps_[:, 0:nq], func=AF.Exp, scale=0.125), reads=[psr], writes=[Pr])
                            act = [g for g in groups if ki < len(g[2])]

                            def av(e, P=P, act=act):
                                for g in act:
                                    kt = g[2][ki]
                                    for j in range(g[1]):
                                        qi = g[0] + j
                                        bank = pO[qi // 2]; off = (qi % 2) * 130
                                        ins = e.matmul(bank[:, off:off + 129], lhsT=P[:, qi * 128:(qi + 1) * 128], rhs=vv[:, kt, 0:129],
                                                       start=False, stop=(ki == len(g[2]) - 1), skip_group_check=True)
                                return ins
                            A("pe", av, reads=[Pr, "vones"] + [("vv", g[2][ki]) for g in act], writes=pO_res[0:nbk])
                            if ki % 9 == 6:
                                ada_next(1)
                            if c == 0 and prev is not None and ki == min(3, maxlen - 1):
                                prev[0]()
                        if c == 0 and prev is not None:
                            prev[1]()
                        for b in range(nbk):
                            A("dve", lambda e, b=b, c=c: e.tensor_copy(out=o_raw[c][:, 2 * b:2 * b + 2, :].rearrange("p a b -> p (a b)"), in_=pO[b][:, 0:260]),
                              reads=[pO_res[b]], writes=[("oraw", c, b)])
                    ORAW = [("oraw", c, b) for c in range(2) for b in range(nbk)]
                    sb_, sbr = st_b.next()
                    for c in range(2):
                        A("dve", lambda e, c=c: e.reciprocal(out=sb_[:, 4 * c:4 * c + nqt], in_=o_raw[c][:, 0:nqt, 128]), reads=ORAW, writes=[(sbr, c)])
                    A("dve", lambda e: e.tensor_scalar(out=sb_[:, 4:4 + nqt], in0=sb_[:, 4:4 + nqt], scalar1=neglam[:, 0:1], scalar2=None, op0=ALU.mult),
                      reads=[(sbr, 1), "neglam"], writes=[(sbr, 1)])
                    for qi in range(nqt):
                        A("dve", lambda e, qi=qi: e.tensor_scalar(out=a0_t[:, qi, :], in0=o_raw[0][:, qi, 0:128], scalar1=sb_[:, qi:qi + 1], scalar2=None, op0=ALU.mult),
                          reads=ORAW + [(sbr, 0)], writes=[("a0", qi)])
                        A("dve", lambda e, qi=qi: e.scalar_tensor_tensor(out=o_t[:, qi, :], in0=o_raw[1][:, qi, 0:128], scalar=sb_[:, 4 + qi:5 + qi], in1=a0_t[:, qi, :],
                                                                        op0=ALU.mult, op1=ALU.add),
                          reads=ORAW + [(sbr, 1), ("a0", qi)], writes=[("o_t", qi)])
                    OT = [("o_t", qi) for qi in range(nqt)]
                    A("dve", lambda e: e.tensor_tensor(out=a0_t[:, 0:nqt, :], in0=o_t[:, 0:nqt, :], in1=o_t[:, 0:nqt, :], op=ALU.mult),
                      reads=OT, writes=[("a0", qi) for qi in range(nqt)])
                    sc_, scr = st_a.next()
                    A("dve", lambda e: e.reduce_sum(out=sc_[:, 0:nqt], in_=a0_t[:, 0:nqt, :], axis=mybir.AxisListType.X),
                      reads=[("a0", qi) for qi in range(nqt)], writes=[scr])

                    def fin2():
                        A("act", lambda e: e.activation(out=sc_[:, 4:4 + nqt], in_=sc_[:, 0:nqt], func=AF.Ln, bias=epsc[:, 0:1], scale=1.0 / 128), reads=[scr, "epsc"], writes=[scr])
                        A("act", lambda e: e.activation(out=sc_[:, 4:4 + nqt], in_=sc_[:, 4:4 + nqt], func=AF.Exp, scale=-0.5), reads=[scr], writes=[scr])
                        for qi in range(nqt):
                            A("dve", lambda e, qi=qi: e.scalar_tensor_tensor(out=on_t[:, qi, :], in0=o_t[:, qi, :], scalar=sc_[:, 4 + qi:5 + qi], in1=subw[:], op0=ALU.mult, op1=ALU.mult),
                              reads=[("o_t", qi), scr, "subw"], writes=[("on_t", qi)])

                    def tail():
                        pt, pr = pT.next()

                        def tr(e):
                            for qi in range(nqt):
                                ins = e.transpose(pt[:, qi * 128:(qi + 1) * 128], on_t[:, qi, :], identb[:])
                            return ins
                        A("pe", tr, reads=[("on_t", qi) for qi in range(nqt)] + ["identb"], writes=[pr])
                        A("act", lambda e: e.activation(out=attT[:, h, q0:q0 + nq], in_=pt[:, 0:nq], func=AF.Copy),
                          reads=[pr], writes=[("attT", h, q0)])
                    return fin2, tail

                def v_block(h, wv, wvr, tb):
                    pt, pr = proj_block(wv, wvr, tb, ring4=True)
                    vb, vbr = vb_ring.next()
                    A("act", lambda e: e.activation(out=vb[:], in_=pt[:], func=AF.Copy), reads=[pr], writes=[vbr])
                    if tb == 0:
                        p4, p4r = pS.next()

                        def mmv(e):
                            for j in range(4):
                                for kc in range(16):
                                    ins = e.matmul(p4[:, j * 128:(j + 1) * 128], lhsT=hTs(kc, j * 128, 128), rhs=wv[:, kc, :], start=(kc == 0), stop=(kc == 15))
                            return ins
                        A("pe", mmv, reads=[wvr] + [("hT", j) for j in range(4)], writes=[p4r])
                        so, sor = so_ring.next()
                        A("act", lambda e: e.activation(out=so[:].rearrange("p a b -> p (a b)"), in_=p4[:], func=AF.Copy), reads=[p4r], writes=[sor])
                        A("sp", lambda e: e.dma_start(out=sv[:, h * 128:(h + 1) * 128].rearrange("(t p) n -> p t n", p=128), in_=so[:]),
                          reads=[sor], key=sor)
                    ptb, ptbr = pT.next()

                    def trv(e):
                        for j in range(4):
                            ins = e.transpose(ptb[:, j * 128:(j + 1) * 128], vb[:, j * 128:(j + 1) * 128], identb[:])
                        return ins
                    A("pe", trv, reads=[vbr, "identb"], writes=[ptbr])
                    for j in range(4):
                        A("dve", lambda e, j=j: e.tensor_copy(out=vv[:, tb * 4 + j, 0:128], in_=ptb[:, j * 128:(j + 1) * 128]),
                          reads=[ptbr], writes=[("vv", tb * 4 + j)])

                prev = None
                for h in range(8):
                    wq, wqr = wsm.next()
                    A("pool", lambda e, wq=wq, h=h: e.dma_start(out=wq[:], in_=w_in_v[:, :, h * 128:(h + 1) * 128]), writes=[wqr], key=wqr)
                    wk, wkr = wsm.next()
                    A("pool", lambda e, wk=wk, h=h: e.dma_start(out=wk[:], in_=w_in_v[:, :, 1024 + h * 128:1024 + (h + 1) * 128]), writes=[wkr], key=wkr)
                    wv, wvr = wsm.next()
                    A("pool", lambda e, wv=wv, h=h: e.dma_start(out=wv[:], in_=w_in_v[:, :, 2048 + h * 128:2048 + (h + 1) * 128]), writes=[wvr], key=wvr)
                    A("pool", lambda e, h=h: e.dma_start(out=ckb[:], in_=ck[:, h * 128:(h + 1) * 128].rearrange("(t p) n -> p t n", p=128)), writes=["ckb"], key="ckb")
                    A("pool", lambda e, h=h: e.dma_start(out=cvb[:], in_=cv[:, h * 128:(h + 1) * 128].rearrange("(t p) n -> p t n", p=128)), writes=["cvb"], key="cvb")
                    blocks = ([dict(kind="k", tb=tb, wt=wk, wr=wkr) for tb in (1, 2, 3, 4)] + [dict(kind="q", tb=tb, wt=wq, wr=wqr) for tb in (1, 2)]
                              + [dict(kind="q", tb=0, wt=wq, wr=wqr), dict(kind="k", tb=0, wt=wk, wr=wkr)])
                    vorder = [1, 2, 3, 4, 0]
                    nb_ = len(blocks)
                    vdone = 0
                    for step in range(nb_ + 2):
                        if step < nb_:
                            stA(blocks[step])
                        if step == 1 and prev is not None:
                            prev[0]()
                            prev[1]()
                            prev = None
                        if 0 <= step - 1 < nb_:
                            stB(blocks[step - 1], h)
                        if 0 <= step - 2 < nb_:
                            stC(blocks[step - 2], h)
                        if step >= 2 and vdone < 5:
                            v_block(h, wv, wvr, vorder[vdone])
                            vdone += 1
                    while vdone < 5:
                        v_block(h, wv, wvr, vorder[vdone])
                        vdone += 1
                    ptb, ptbr = pT.next()

                    def trc(e, ptb=ptb):
                        for j in range(2):
                            ins = e.transpose(ptb[:, j * 128:(j + 1) * 128], ckb[:, j, :], identb[:])
                        return ins
                    A("pe", trc, reads=["ckb", "identb"], writes=[ptbr])
                    A("dve", lambda e, ptb=ptb: e.tensor_copy(out=kT[0][0:64, NALL:NKEY], in_=ptb[0:64, 0:256]), reads=[ptbr], writes=[("kT0", 5)])
                    A("dve", lambda e, ptb=ptb: e.tensor_copy(out=kT[1][64:128, NALL:NKEY], in_=ptb[64:128, 0:256]), reads=[ptbr], writes=[("kT1", 5)])
                    for j in range(2):
                        A("dve", lambda e, j=j: e.tensor_copy(out=vv[:, 20 + j, 0:128], in_=cvb[:, j, :]), reads=["cvb"], writes=[("vv", 20 + j)])
                    prev = attention(h, 512, 512, [(0, 4, list(range(4, 22)))], prev)
                    prev = attention(h, 1024, 512, [(0, 4, list(range(4, 22)))], prev)
                    prev = attention(h, 0, 512, [(0, 2, [0, 1]), (2, 2, [2, 3])], prev)
                    if h == 0:
                        chk(2)
                prev[0]()
                prev[1]()
                ada_next(100)
                S.barrier()
            chk(3)

            with ExitStack() as pg_:
                wg = sbt(pg_, "wg", [128, 16, 1024], BF16)
                gn = sbt(pg_, "gn", [128, 12, 1024], BF16)
                sgw = sbt(pg_, "sgw", [128, 1024]); wsb = sbt(pg_, "wsb", [128, 1024], BF16); bsf = sbt(pg_, "bsf", [1, 1024]); onesf = sbt(pg_, "onesf", [1, 128])
                gg_ring = Ring("gg", [sbt(pg_, "gg%d" % i, [128, 1024]) for i in range(1)])
                gu_ring = Ring("gu", [sbt(pg_, "gu%d" % i, [128, 512], BF16) for i in range(2)])
                wu_ring = Ring("wu", [sbt(pg_, "wu%d" % i, [128, 16, 128], BF16) for i in range(2)])
                for nq_ in range(4):
                    A("pool", lambda e, nq_=nq_: e.dma_start(out=wg[:, :, nq_ * 256:(nq_ + 1) * 256], in_=w_in_v[:, :, 4096 + nq_ * 256:4096 + (nq_ + 1) * 256]),
                      writes=[("wg", nq_)], key=("wg", nq_))
                A("sp", lambda e: e.dma_start(out=sgw[:], in_=sgunw), writes=["sgw"], key="sgw")
                A("pool", lambda e: e.dma_start(out=wsb[:], in_=wsT), writes=["wsb"], key="wsb")
                A("sp", lambda e: e.dma_start(out=bsf[:], in_=bsr), writes=["bsf"], key="bsf")
                A("dve", lambda e: e.memset(onesf[:], 1.0), writes=["onesf"])
                for tt in range(12):
                    gg, ggr = gg_ring.next()
                    sa, sar = st_a.next()
                    for nb in range(2):
                        pt, pr = p6_next()

                        def mm(e, pt=pt, nb=nb, tt=tt):
                            for kc in range(16):
                                ins = e.matmul(pt[:], lhsT=hTs(kc, tt * 128, 128), rhs=wg[:, kc, nb * 512:(nb + 1) * 512], start=(kc == 0), stop=(kc == 15))
                            return ins
                        A("pe", mm, reads=[("wg", 2 * nb), ("wg", 2 * nb + 1), ("hT", tt)], writes=[pr])
                        A("act", lambda e, pt=pt, nb=nb, gg=gg, sa=sa: e.activation(out=gg[:, nb * 512:(nb + 1) * 512], in_=pt[:], func=AF.Gelu_apprx_tanh,
                                                                                   accum_out=sa[:, nb:nb + 1]),
                          reads=[pr], writes=[(ggr, nb), (sar, nb)])
                    gjr = ("gn", tt)
                    A("act", lambda e, gg=gg, sa=sa, tt=tt: e.activation(out=gn[:, tt, :], in_=gg[:], func=AF.Square, accum_out=sa[:, 2:3]),
                      reads=[(ggr, 0), (ggr, 1)], writes=[gjr, (sar, 2)])
                    A("dve", lambda e, sa=sa: e.tensor_tensor(out=sa[:, 3:4], in0=sa[:, 0:1], in1=sa[:, 1:2], op=ALU.add), reads=[(sar, 0), (sar, 1)], writes=[(sar, 3)])
                    A("dve", lambda e, sa=sa: e.tensor_scalar(out=sa[:, 3:4], in0=sa[:, 3:4], scalar1=1.0 / 1024, scalar2=None, op0=ALU.mult), reads=[(sar, 3)], writes=[(sar, 3)])
                    A("dve", lambda e, sa=sa: e.tensor_tensor(out=sa[:, 4:5], in0=sa[:, 3:4], in1=sa[:, 3:4], op=ALU.mult), reads=[(sar, 3)], writes=[(sar, 4)])
                    A("dve", lambda e, sa=sa: e.scalar_tensor_tensor(out=sa[:, 5:6], in0=sa[:, 2:3], scalar=1.0 / 1024, in1=sa[:, 4:5], op0=ALU.mult, op1=ALU.subtract),
                      reads=[(sar, 2), (sar, 4)], writes=[(sar, 5)])
                    A("act", lambda e, sa=sa: e.activation(out=sa[:, 6:7], in_=sa[:, 5:6], func=AF.Sqrt, bias=epsc[:, 0:1], scale=1.0), reads=[(sar, 5), "epsc"], writes=[(sar, 6)])
                    A("dve", lambda e, sa=sa: e.reciprocal(out=sa[:, 6:7], in_=sa[:, 6:7]), reads=[(sar, 6)], writes=[(sar, 6)])
                    A("dve", lambda e, sa=sa: e.scalar_tensor_tensor(out=sa[:, 7:8], in0=sa[:, 3:4], scalar=-1.0, in1=sa[:, 6:7], op0=ALU.mult, op1=ALU.mult),
                      reads=[(sar, 3), (sar, 6)], writes=[(sar, 7)])
                    A("act", lambda e, gg=gg, sa=sa: e.activation(out=gg[:], in_=gg[:], func=AF.Identity, bias=sa[:, 7:8], scale=sa[:, 6:7]),
                      reads=[(ggr, 0), (ggr, 1), (sar, 6), (sar, 7), gjr], writes=[(ggr, 0), (ggr, 1)])
                    A("dve", lambda e, gg=gg, tt=tt: e.tensor_tensor(out=gn[:, tt, :], in0=gg[:], in1=sgw[:], op=ALU.mult), reads=[(ggr, 0), (ggr, 1), "sgw"], writes=[("gn", tt)])
                for gi in range(8):
                    wu, wur = wu_ring.next()
                    A("pool", lambda e, wu=wu, gi=gi: e.dma_start(out=wu[:], in_=w_in_v[:, :, 3072 + gi * 128:3072 + (gi + 1) * 128]), writes=[wur], key=wur)
                    for tb in range(3):
                        pt, pr = p6_next()

                        def mm(e, pt=pt, wu=wu, tb=tb):
                            for kc in range(16):
                                ins = e.matmul(pt[:], lhsT=wu[:, kc, :], rhs=hTs(kc, tb * 512, 512), start=(kc == 0), stop=(kc == 15))
                            return ins
                        A("pe", mm, reads=[wur] + [("hT", tb * 4 + j) for j in range(4)], writes=[pr])
                        gu, gur = gu_ring.next()
                        A("act", lambda e, pt=pt, gu=gu: e.activation(out=gu[:], in_=pt[:], func=AF.Gelu_apprx_tanh), reads=[pr], writes=[gur])
                        pm, pmr = p6_next()

                        def mix(e, pm=pm, gi=gi, tb=tb):
                            for j in range(4):
                                tt = tb * 4 + j
                                e.matmul(pm[:, j * 128:(j + 1) * 128], lhsT=gn[:, tt, gi * 128:(gi + 1) * 128], rhs=wsb[:, gi * 128:(gi + 1) * 128], start=True, stop=False)
                                ins = e.matmul(pm[:, j * 128:(j + 1) * 128], lhsT=onesf[0:1, :], rhs=bsf[0:1, gi * 128:(gi + 1) * 128], start=False, stop=True)
                            return ins
                        A("pe", mix, reads=[("gn", tb * 4 + j) for j in range(4)] + ["wsb", "bsf", "onesf"], writes=[pmr])
                        A("dve", lambda e, pm=pm, gu=gu, gi=gi, tb=tb: e.tensor_tensor(out=sguT_ap(gi, tb * 512, 512), in0=pm[:], in1=gu[:], op=ALU.mult),
                          reads=[pmr, gur], writes=[("sguT", gi, tb)])
                S.barrier()
            chk(4)

        with ExitStack() as pb:
            acc = sbt(pb, "acc", [128, 6, D])
            h2T = sbt(pb, "h2T", [128, 16, HALF], BF16)
            aT = sbt(pb, "aT", [128, 16, HALF], BF16)
            wb = Ring("wb", [sbt(pb, "wb%d" % i, [128, 16, 256], BF16) for i in range(2)])
            xst_ring = Ring("xst", [sbt(pb, "xst%d" % i, [128, 256]) for i in range(3)])
            tmp_ring = Ring("tmp", [sbt(pb, "tmp%d" % i, [128, 256]) for i in range(3)])
            rl_ring = Ring("rl", [sbt(pb, "rl%d" % i, [128, 512], BF16) for i in range(2)])
            xn2_ring = Ring("xn2", [sbt(pb, "xn2%d" % i, [128, D], BF16) for i in range(2)])
            GATES = [("gates", gi, r, nb) for gi in range(2) for r in range(2) for nb in range(4)]
            gates = sbt(pb, "gates", [128, 4, D], BF16)
            for gi in range(2):
                A("sp", lambda e, gi=gi: e.dma_start(out=acc[0:2, gi, :], in_=gsc[2 * gi:2 * gi + 2, :]), writes=[("gstage", gi)], key=("gstage", gi))
                for r in range(2):
                    for nb in range(4):
                        pt, pr = p6_next()
                        A("pe", lambda e, pt=pt, r=r, gi=gi, nb=nb: e.matmul(pt[:], lhsT=sel[0:2, r * 128:(r + 1) * 128], rhs=acc[0:2, gi, nb * 512:(nb + 1) * 512],
                                                                          start=True, stop=True),
                          reads=["sel", ("gstage", gi)], writes=[pr])
                        A("act", lambda e, pt=pt, gi=gi, r=r, nb=nb: e.activation(out=gates[:, gi * 2 + r, nb * 512:(nb + 1) * 512], in_=pt[:], func=AF.Copy),
                          reads=[pr], writes=[("gates", gi, r, nb)])
            S.barrier()
            GATES = []
            for half in range(2):
                t0 = half * 6
                for nb in range(8):
                    wt, wr = wb.next()
                    A("pool", lambda e, wt=wt, nb=nb: e.dma_start(out=wt[:], in_=w_o_v[:, :, nb * 256:(nb + 1) * 256]), writes=[wr], key=wr)
                    for ti in range(6):
                        tt = t0 + ti
                        r = 0 if tt < 4 else 1
                        pt, pr = p6_next()

                        def mm(e, pt=pt, wt=wt, tt=tt):
                            for kc in range(16):
                                lt = attT[:, kc, tt * 128:(tt + 1) * 128] if kc < 8 else sguT_ap(kc - 8, tt * 128, 128)
                                ins = e.matmul(pt[:, 0:256], lhsT=lt, rhs=wt[:, kc, :], start=(kc == 0), stop=(kc == 15))
                            return ins
                        A("pe", mm, reads=[wr] + [("sguT", g_, tt // 4) for g_ in range(8)], writes=[pr])
                        xst, xsr = xst_ring.next()
                        A("sp", lambda e, xst=xst, tt=tt, nb=nb: e.dma_start(out=xst[:], in_=xrows(tt)[:, nb * 256:(nb + 1) * 256]), writes=[xsr], key=xsr)
                        tmp, tmr = tmp_ring.next()
                        A("dve", lambda e, pt=pt, tmp=tmp, r=r, nb=nb: e.tensor_tensor(out=tmp[:], in0=pt[:, 0:256], in1=gates[:, r, nb * 256:(nb + 1) * 256], op=ALU.mult),
                          reads=[pr] + GATES, writes=[tmr])
                        A("dve", lambda e, tmp=tmp, xst=xst, ti=ti, nb=nb: e.tensor_tensor(out=acc[:, ti, nb * 256:(nb + 1) * 256], in0=tmp[:], in1=xst[:], op=ALU.add),
                          reads=[tmr, xsr], writes=[("acc", ti)])
                h2d = (lambda kc, c0, n: h2T[:, kc, c0:c0 + n])
                prev = None
                for ti in range(6):
                    cur = norm_stats(acc[:, ti, :], ("acc", ti), xn2_ring) + (ti,)
                    if prev is not None:
                        norm_tr(prev[0], prev[1], A2, S2, 0 if t0 + prev[2] < 4 else 1, h2d, ("h2T", prev[2]), prev[2] * 128)
                    prev = cur
                norm_tr(prev[0], prev[1], A2, S2, 0 if t0 + prev[2] < 4 else 1, h2d, ("h2T", prev[2]), prev[2] * 128)
                for rd in range(4):
                    for fb in range(8):
                        wt, wr = wb.next()
                        f0 = rd * 2048 + fb * 256
                        A("pool", lambda e, wt=wt, f0=f0: e.dma_start(out=wt[:], in_=w_ff1_v[:, :, f0:f0 + 256]), writes=[wr], key=wr)
                        for fj in range(2):
                            fc = fb * 2 + fj
                            for (c0, cn) in ((0, 512), (512, 256)):
                                pt, pr = p6_next()

                                def mm(e, pt=pt, wt=wt, fj=fj, c0=c0, cn=cn):
                                    for kc in range(16):
                                        ins = e.matmul(pt[:, 0:cn], lhsT=wt[:, kc, fj * 128:(fj + 1) * 128], rhs=h2T[:, kc, c0:c0 + cn], start=(kc == 0), stop=(kc == 15))
                                    return ins
                                A("pe", mm, reads=[wr] + [(("h2T", ti), kc) for ti in range(c0 // 128, (c0 + cn) // 128) for kc in range(16)], writes=[pr])
                                rl, rlr = rl_ring.next()
                                A("act", lambda e, pt=pt, rl=rl, cn=cn: e.activation(out=rl[:, 0:cn], in_=pt[:, 0:cn], func=AF.Relu), reads=[pr], writes=[rlr])
                                A("dve", lambda e, rl=rl, fc=fc, c0=c0, cn=cn: e.tensor_tensor(out=aT[:, fc, c0:c0 + cn], in0=rl[:, 0:cn], in1=rl[:, 0:cn], op=ALU.mult),
                                  reads=[rlr], writes=[("aT", fc, c0)])
                    for nb in range(8):
                        wt, wr = wb.next()
                        A("pool", lambda e, wt=wt, rd=rd, nb=nb: e.dma_start(out=wt[:], in_=w_ff2_v[:, rd * 16:(rd + 1) * 16, nb * 256:(nb + 1) * 256]), writes=[wr], key=wr)
                        for ti in range(6):
                            tt = t0 + ti
                            r = 0 if tt < 4 else 1
                            pt, pr = p6_next()

                            def mm(e, pt=pt, wt=wt, ti=ti):
                                for fc in range(16):
                                    ins = e.matmul(pt[:, 0:256], lhsT=aT[:, fc, ti * 128:(ti + 1) * 128], rhs=wt[:, fc, :], start=(fc == 0), stop=(fc == 15))
                                return ins
                            A("pe", mm, reads=[wr] + [("aT", fc, 0 if ti < 4 else 512) for fc in range(16)], writes=[pr])
                            tmp, tmr = tmp_ring.next()
                            A("dve", lambda e, pt=pt, tmp=tmp, r=r, nb=nb: e.tensor_tensor(out=tmp[:], in0=pt[:, 0:256], in1=gates[:, 2 + r, nb * 256:(nb + 1) * 256], op=ALU.mult),
                              reads=[pr] + GATES, writes=[tmr])
                            A("dve", lambda e, tmp=tmp, ti=ti, nb=nb: e.tensor_tensor(out=acc[:, ti, nb * 256:(nb + 1) * 256], in0=acc[:, ti, nb * 256:(nb + 1) * 256], in1=tmp[:], op=ALU.add),
                              reads=[tmr, ("acc", ti)], writes=[("acc", ti)])
                for ti in range(6):
                    tt = t0 + ti
                    A("sp", lambda e, ti=ti, tt=tt: e.dma_start(out=yrows(tt), in_=acc[:, ti, :]), reads=[("acc", ti)], key=("yout", ti))
                chk(5 + half)
            S.barrier(engines=("sp",))
    return nc


def _rope_tables(order):
    n_freq = 16
    freqs = (10000.0 ** (-np.arange(n_freq, dtype=np.float32) / n_freq)).astype(np.float32)
    r = (order // 64).astype(np.float32)
    col = (order % 64).astype(np.float32)
    ang_r = r[:, None] * freqs
    ang_c = col[:, None] * freqs
    ang = np.concatenate([ang_r, ang_r, ang_c, ang_c], axis=-1)
    cos = np.cos(ang).astype(np.float32).T
    sin = np.sin(ang).astype(np.float32).T
    return np.ascontiguousarray(np.concatenate([cos, cos], 0)), np.ascontiguousarray(np.concatenate([sin, sin], 0))


def _consts():
    ident = np.eye(128, dtype=np.float32)
    bones = np.zeros((128, 128), np.float32)
    bones[0:64, 0:64] = 1.0 / 64
    bones[64:128, 64:128] = 1.0 / 64
    perm = np.zeros((128, 128), np.float32)
    for d in range(128):
        if d % 32 < 16:
            perm[d + 16, d] = -1.0
        else:
            perm[d - 16, d] = 1.0
    sel = np.zeros((2, 256), np.float32)
    sel[0, 0:128] = 1.0
    sel[1, 128:256] = 1.0
    return ident, bones, perm, sel


_NC_CACHE = {}


def kernel(x_prompt, x_sample, cache_k, cache_v, c, c_ctx, w_ada, b_ada, norm1_w, norm2_w, w_in,
           q_norm_w, k_norm_w, lambda_q1, lambda_k1, lambda_q2, lambda_k2, subln_w, sgu_norm_w,
           w_s, b_s, w_o, w_ff1, w_ff2):
    f = lambda a: np.ascontiguousarray(np.asarray(a, dtype=np.float32))
    x_prompt, x_sample, cache_k, cache_v, c, c_ctx = map(f, (x_prompt, x_sample, cache_k, cache_v, c, c_ctx))
    ident, bones, perm, sel = _consts()
    col = lambda v: np.ascontiguousarray(f(v).reshape(16, 128).T)
    shared = {
        "w_ada": f(w_ada)[0], "bada2": np.ascontiguousarray(np.stack([f(b_ada)[0]] * 2)),
        "nw1": col(norm1_w[0]), "nw2": col(norm2_w[0]), "w_in": f(w_in)[0],
        "qkw": np.ascontiguousarray(np.stack([np.tile(f(q_norm_w)[0], 2), np.tile(f(k_norm_w)[0], 2)], axis=1)),
        "lamv": np.ascontiguousarray(np.broadcast_to(np.concatenate([f(lambda_q1)[0], f(lambda_k1)[0], f(lambda_q2)[0], f(lambda_k2)[0]])[None, :], (128, 256))),
        "sublnw": np.ascontiguousarray(np.broadcast_to(f(subln_w)[0][None, :], (128, 128))),
        "sgunw": np.ascontiguousarray(np.broadcast_to(f(sgu_norm_w)[0][None, :], (128, 1024))),
        "wsT": np.ascontiguousarray(f(w_s)[0].transpose(2, 0, 1).reshape(128, 1024)),
        "bsr": np.ascontiguousarray(f(b_s)[0].reshape(1, 1024)),
        "w_o": f(w_o)[0], "w_ff1": f(w_ff1)[0], "w_ff2": f(w_ff2)[0],
        "c_ident": ident, "c_bones": bones, "c_perm": perm, "c_sel": sel,
    }
    in_maps = []
    for i in range(NCORES):
        b, s = i // 2, i % 2
        own = np.arange(s * 1024, (s + 1) * 1024)
        oth = np.arange((1 - s) * 1024, (2 - s) * 1024)
        order = np.concatenate([own, oth])
        cos, sin = _rope_tables(order)
        cvec = np.stack([c_ctx, c[b]], axis=0)
        cT = np.ascontiguousarray(cvec.reshape(2, 16, 128).transpose(2, 1, 0).reshape(128, 32))
        m = dict(shared)
        m.update({
            "xp": np.ascontiguousarray(x_prompt[2 * i:2 * i + 2].reshape(512, D)),
            "xs": np.ascontiguousarray(x_sample[b][order]),
            "ck": np.ascontiguousarray(cache_k[b, 0].reshape(256, 1024)),
            "cv": np.ascontiguousarray(cache_v[b, 0].reshape(256, 1024)),
            "cT": cT, "c_cos": cos, "c_sin": sin,
        })
        in_maps.append(m)
    if "nc" not in _NC_CACHE:
        _NC_CACHE["nc"] = build_program()
    res = run_bass_kernel_spmd(_NC_CACHE["nc"], in_maps, core_ids=list(range(NCORES)))
    outs = res.results
    y_p = np.zeros((16, 256, D), np.float32)
    y_s = np.zeros((4, 2048, D), np.float32)
    s_k = np.zeros((16, 1, 256, 8, 2, 64), np.float32)
    s_v = np.zeros((16, 1, 256, 8, 128), np.float32)
    for i in range(NCORES):
        b, s = i // 2, i % 2
        y_p[2 * i:2 * i + 2] = outs[i]["yp"].reshape(2, 256, D)
        y_s[b, s * 1024:(s + 1) * 1024] = outs[i]["ys"]
        s_k[2 * i:2 * i + 2, 0] = outs[i]["sk"].reshape(2, 256, 8, 2, 64)
        s_v[2 * i:2 * i + 2, 0] = outs[i]["sv"].reshape(2, 256, 8, 128)
    return (y_p, y_s, s_k, s_v)
```

```python
import math
from contextlib import ExitStack

import numpy as np
import concourse.bass as bass
import concourse.mybir as mybir
from concourse.bass_utils import run_bass_kernel_spmd

F32 = mybir.dt.float32
BF16 = mybir.dt.bfloat16
AF = mybir.ActivationFunctionType
ALU = mybir.AluOpType

D = 2048
NCORES = 8
EPS = 1e-6
LAM_INIT = 0.8 - 0.6 * math.exp(-0.3 * 0)
NTOK = 1536
NALL = 2560
NKEY = 2816
HALF = 768


class _Stop(Exception):
    pass


KSTOP = [99]
DBG = {}


class _Node:
    __slots__ = ("eng", "signal", "dma")


class Sched:
    def __init__(self, nc, es):
        self.nc = nc
        self.es = es
        self.eng = {"pe": nc.tensor, "act": nc.scalar, "dve": nc.vector, "pool": nc.gpsimd, "sp": nc.sync}
        self.esem = {e: es.enter_context(nc.semaphore("c_" + e)) for e in self.eng}
        self.cnt = {e: 0 for e in self.eng}
        self.dsem = {}
        self.dcnt = {}
        self.waited = {e: {} for e in self.eng}
        self.last_w = {}
        self.readers = {}

    def _wait(self, eng, sem_name, sem, val):
        if self.waited[eng].get(sem_name, 0) >= val:
            return
        self.eng[eng].wait_ge(sem, val)
        self.waited[eng][sem_name] = val

    def add(self, eng, fn, reads=(), writes=(), key=None):
        deps = {}
        for r in reads:
            w = self.last_w.get(r)
            if w is not None:
                deps[id(w)] = w
        for r in writes:
            w = self.last_w.get(r)
            if w is not None:
                deps[id(w)] = w
            for rd in self.readers.get(r, ()):
                deps[id(rd)] = rd
        for d in deps.values():
            if d.eng == "pe" and eng == "pe" and not d.dma and key is None:
                continue
            self._wait(eng, d.signal[0], d.signal[1], d.signal[2])
        ins = fn(self.eng[eng])
        node = _Node()
        node.eng = eng
        node.dma = key is not None
        if key is not None:
            name = "d_" + str(key)
            if name not in self.dsem:
                self.dsem[name] = self.es.enter_context(self.nc.semaphore(name))
                self.dcnt[name] = 0
            self.dcnt[name] += 1
            ins.then_inc(self.dsem[name], 16)
            node.signal = (name, self.dsem[name], 16 * self.dcnt[name])
        else:
            self.cnt[eng] += 1
            ins.then_inc(self.esem[eng], 1)
            node.signal = ("c_" + eng, self.esem[eng], self.cnt[eng])
        for r in reads:
            self.readers.setdefault(r, []).append(node)
        for r in writes:
            self.last_w[r] = node
            self.readers[r] = []
        return node

    def barrier(self, engines=("pe", "act", "dve", "pool", "sp")):
        for e in engines:
            for f in self.eng:
                if self.cnt[f] > 0:
                    self._wait(e, "c_" + f, self.esem[f], self.cnt[f])
            for name, sem in self.dsem.items():
                self._wait(e, name, sem, 16 * self.dcnt[name])
        self.last_w = {}
        self.readers = {}


class Ring:
    def __init__(self, name, tiles):
        self.name = name
        self.tiles = tiles
        self.i = 0

    def next(self):
        j = self.i % len(self.tiles)
        self.i += 1
        return self.tiles[j], (self.name, j)


def build_program():
    nc = bass.Bass("TRN2", target_bir_lowering=False)

    def din(name, shape):
        return nc.dram_tensor(name, list(shape), F32, kind="ExternalInput").ap()

    def dout(name, shape):
        return nc.dram_tensor(name, list(shape), F32, kind="ExternalOutput").ap()

    xp = din("xp", [512, D]); xs = din("xs", [2048, D])
    ck = din("ck", [256, 1024]); cv = din("cv", [256, 1024])
    cT = din("cT", [128, 32]); w_ada = din("w_ada", [D, 6 * D]); bada2 = din("bada2", [2, 6 * D])
    nw1 = din("nw1", [128, 16]); nw2 = din("nw2", [128, 16])
    w_in = din("w_in", [D, 5120]); qkw = din("qkw", [128, 2]); lamv = din("lamv", [128, 256])
    sublnw = din("sublnw", [128, 128]); sgunw = din("sgunw", [128, 1024])
    wsT = din("wsT", [128, 1024]); bsr = din("bsr", [1, 1024])
    w_o = din("w_o", [D, D]); w_ff1 = din("w_ff1", [D, 4 * D]); w_ff2 = din("w_ff2", [4 * D, D])
    c_ident = din("c_ident", [128, 128]); c_bones = din("c_bones", [128, 128]); c_perm = din("c_perm", [128, 128])
    c_sel = din("c_sel", [2, 256]); c_cos = din("c_cos", [128, 2048]); c_sin = din("c_sin", [128, 2048])
    yp = dout("yp", [512, D]); ys = dout("ys", [1024, D]); sk = dout("sk", [512, 1024]); sv = dout("sv", [512, 1024])

    w_ada_v = w_ada.rearrange("(k p) n -> p k n", p=128)
    gsc = nc.dram_tensor("gate_rows", [4, D], F32)
    w_in_v = w_in.rearrange("(k p) n -> p k n", p=128)
    w_o_v = w_o.rearrange("(k p) n -> p k n", p=128)
    w_ff1_v = w_ff1.rearrange("(k p) n -> p k n", p=128)
    w_ff2_v = w_ff2.rearrange("(k p) n -> p k n", p=128)

    def xrows(tt):
        if tt < 4:
            return xp[tt * 128:(tt + 1) * 128, :]
        return xs[(tt - 4) * 128:(tt - 3) * 128, :]

    def yrows(tt):
        if tt < 4:
            return yp[tt * 128:(tt + 1) * 128, :]
        return ys[(tt - 4) * 128:(tt - 3) * 128, :]

    top = ExitStack()
    try:
        _body(nc, top, locals())
    except _Stop:
        pass
    return nc


def _body(nc, top, L):
    globals().update({k: v for k, v in L.items() if k not in ("nc", "top")})
    with top:
        S = Sched(nc, top)
        A = S.add

        def sbt(es, name, shape, dt=F32):
            return es.enter_context(nc.sbuf_tensor(name, list(shape), dt))

        def pst(es, name, shape, dt=F32):
            return es.enter_context(nc.psum_tensor(name, list(shape), dt))

        pg = Ring("pg", [pst(top, "pg%d" % i, [128, 512]) for i in range(2)])
        pS = Ring("pS", [pst(top, "pS%d" % i, [128, 512]) for i in range(2)])
        pO = [pst(top, "pO%d" % i, [128, 512]) for i in range(2)]
        pT = Ring("pT", [pst(top, "pT%d" % i, [128, 1024], BF16) for i in range(2)])
        p6_tiles = [pg.tiles[0], pg.tiles[1], pS.tiles[0], pS.tiles[1], pO[0], pO[1]]
        p6_names = [("pg", 0), ("pg", 1), ("pS", 0), ("pS", 1), ("pO", 0), ("pO", 1)]
        p6_i = [0]

        def p6_next():
            j = p6_i[0] % 6
            p6_i[0] += 1
            return p6_tiles[j], p6_names[j]

        identb = sbt(top, "identb", [128, 128], BF16)
        identf = sbt(top, "identf", [128, 128])
        bones = sbt(top, "bones", [128, 128], BF16)
        perm = sbt(top, "perm", [128, 128], BF16)
        sel = sbt(top, "sel", [2, 256])
        zerob = sbt(top, "zerob", [128, 260], BF16)
        epsc = sbt(top, "epsc", [128, 1])
        A1 = sbt(top, "A1", [128, 16, 2]); S1 = sbt(top, "S1", [128, 16, 2])
        A2 = sbt(top, "A2", [128, 16, 2]); S2 = sbt(top, "S2", [128, 16, 2])
        scT = sbt(top, "scT", [128, 16, 2], BF16)
        nw2_t = sbt(top, "nw2_t", [128, 16])
        attT = sbt(top, "attT", [128, 8, NTOK], BF16)
        shr = sbt(top, "shr", [128, 16384], BF16)

        def sguT_ap(g, c0, n):
            return shr[:, g * NTOK + c0:g * NTOK + c0 + n]
        qkw_t = sbt(top, "qkw_t", [128, 2])
        neglam = sbt(top, "neglam", [128, 1])
        subw = sbt(top, "subw", [128, 128])
        st_a = Ring("st_a", [sbt(top, "st_a%d" % i, [128, 8]) for i in range(4)])
        st_b = Ring("st_b", [sbt(top, "st_b%d" % i, [128, 8]) for i in range(4)])

        A("pool", lambda e: e.dma_start(out=identb[:], in_=c_ident), writes=["identb"], key="identb")
        A("sp", lambda e: e.dma_start(out=identf[:], in_=c_ident), writes=["identf"], key="identf")
        A("pool", lambda e: e.dma_start(out=bones[:], in_=c_bones), writes=["bones"], key="bones")
        A("pool", lambda e: e.dma_start(out=perm[:], in_=c_perm), writes=["perm"], key="perm")
        A("sp", lambda e: e.dma_start(out=sel[:], in_=c_sel), writes=["sel"], key="sel")
        A("sp", lambda e: e.dma_start(out=qkw_t[:], in_=qkw), writes=["qkw"], key="qkw")
        A("sp", lambda e: e.dma_start(out=subw[:], in_=sublnw), writes=["subw"], key="subw")
        A("dve", lambda e: e.memset(zerob[:], 0.0), writes=["zerob"])
        A("dve", lambda e: e.memset(epsc[:], EPS), writes=["epsc"])
        A("dve", lambda e: e.tensor_scalar(out=subw[:], in0=subw[:], scalar1=1.0 - LAM_INIT, scalar2=None, op0=ALU.mult),
          reads=["subw"], writes=["subw"])

        def chk(phase):
            if KSTOP[0] <= phase:
                S.barrier(engines=("sp",))
                raise _Stop()

        with ExitStack() as p0:
            lam_t = sbt(p0, "lam_t", [128, 256]); lam_p = sbt(p0, "lam_p", [128, 128]); lam_s = sbt(p0, "lam_s", [128, 4])
            A("sp", lambda e: e.dma_start(out=lam_t[:], in_=lamv), writes=["lam_t"], key="lam_t")
            for j in range(2):
                A("dve", lambda e, j=j: e.tensor_tensor(out=lam_p[:, j * 64:(j + 1) * 64], in0=lam_t[:, (2 * j) * 64:(2 * j + 1) * 64],
                                                          in1=lam_t[:, (2 * j + 1) * 64:(2 * j + 2) * 64], op=ALU.mult),
                  reads=["lam_t"], writes=[("lam_p", j)])
                A("dve", lambda e, j=j: e.reduce_sum(out=lam_s[:, j:j + 1], in_=lam_p[:, j * 64:(j + 1) * 64], axis=mybir.AxisListType.X),
                  reads=[("lam_p", j)], writes=[("lam_s", j)])
                A("act", lambda e, j=j: e.activation(out=lam_s[:, 2 + j:3 + j], in_=lam_s[:, j:j + 1], func=AF.Exp),
                  reads=[("lam_s", j)], writes=[("lam_e", j)])
            A("dve", lambda e: e.tensor_tensor(out=neglam[:], in0=lam_s[:, 3:4], in1=lam_s[:, 2:3], op=ALU.subtract),
              reads=[("lam_e", 0), ("lam_e", 1)], writes=["neglam"])
            A("dve", lambda e: e.tensor_scalar(out=neglam[:], in0=neglam[:], scalar1=-LAM_INIT, scalar2=None, op0=ALU.add),
              reads=["neglam"], writes=["neglam"])

            cT_t = sbt(p0, "cT_t", [128, 32])
            mod = sbt(p0, "mod", [2, 2 * D])
            nw1_t = sbt(p0, "nw1_t", [128, 16])
            wa = Ring("wa", [sbt(p0, "wa%d" % i, [128, 16, 512], BF16) for i in range(3)])
            A("sp", lambda e: e.dma_start(out=cT_t[:], in_=cT), writes=["cT_t"], key="cT_t")
            A("sp", lambda e: e.dma_start(out=mod[:], in_=bada2[:, 0:2 * D]), writes=[("mod", b) for b in range(8)], key="bada_t")
            A("sp", lambda e: e.dma_start(out=nw1_t[:], in_=nw1), writes=["nw1_t"], key="nw1_t")
            A("sp", lambda e: e.dma_start(out=nw2_t[:], in_=nw2), writes=["nw2_t"], key="nw2_t")
            A("act", lambda e: e.activation(out=scT[:].rearrange("p k r -> p (k r)"), in_=cT_t[:], func=AF.Silu), reads=["cT_t"], writes=["scT"])
            for blk in range(8):
                wt, wr = wa.next()
                A("pool", lambda e, wt=wt, blk=blk: e.dma_start(out=wt[:], in_=w_ada_v[:, :, blk * 512:(blk + 1) * 512]), writes=[wr], key=wr)
                pt, pr = pg.next()

                def mm(e, wt=wt, pt=pt):
                    for kc in range(16):
                        ins = e.matmul(pt[0:2, :], lhsT=scT[:, kc, :], rhs=wt[:, kc, :], start=(kc == 0), stop=(kc == 15))
                    return ins
                A("pe", mm, reads=[wr, "scT"], writes=[pr])
                A("dve", lambda e, pt=pt, blk=blk: e.tensor_tensor(out=mod[0:2, blk * 512:(blk + 1) * 512], in0=pt[0:2, :],
                                                                    in1=mod[0:2, blk * 512:(blk + 1) * 512], op=ALU.add),
                  reads=[pr, ("mod", blk)], writes=[("mod", blk)])
            for ci, (chunk, dst) in enumerate([(0, S1), (1, A1)]):
                pt, pr = pg.next()

                def tr(e, pt=pt, chunk=chunk):
                    for kc in range(16):
                        c0 = chunk * D + kc * 128
                        ins = e.transpose(pt[:, kc * 2:kc * 2 + 2], mod[0:2, c0:c0 + 128], identf[0:2, 0:2])
                    return ins
                A("pe", tr, reads=[("mod", b) for b in range(chunk * 4, chunk * 4 + 4)] + ["identf"], writes=[pr])
                A("dve", lambda e, pt=pt, dst=dst: e.tensor_copy(out=dst[:].rearrange("p k r -> p (k r)"), in_=pt[:, 0:32]),
                  reads=[pr], writes=[("modc", ci)])
            for r in range(2):
                A("dve", lambda e, r=r: e.scalar_tensor_tensor(out=A1[:, :, r], in0=A1[:, :, r], scalar=1.0, in1=nw1_t[:], op0=ALU.add, op1=ALU.mult),
                  reads=[("modc", 1), "nw1_t"], writes=[("modc", 1)])
            S.barrier()
        chk(0)
        MODC = [("modc", i) for i in range(2)]

        def norm_stats(src_ap, src_res, xn_ring):
            sa, sar = st_a.next()
            xn, xnr = xn_ring.next()
            A("act", lambda e: e.activation(out=xn[:], in_=src_ap, func=AF.Square, accum_out=sa[:, 0:1]), reads=[src_res], writes=[xnr, sar])
            A("act", lambda e: e.activation(out=sa[:, 1:2], in_=sa[:, 0:1], func=AF.Sqrt, bias=epsc[:, 0:1], scale=1.0 / D),
              reads=[sar, "epsc"], writes=[sar])
            A("dve", lambda e: e.reciprocal(out=sa[:, 2:3], in_=sa[:, 1:2]), reads=[sar], writes=[sar])
            A("dve", lambda e: e.tensor_scalar(out=xn[:], in0=src_ap, scalar1=sa[:, 2:3], scalar2=None, op0=ALU.mult),
              reads=[src_res, sar], writes=[xnr])
            return xn, xnr

        def norm_tr(xn, xnr, Acol, Scol, r, dst, dst_res, col0):
            for q in range(2):
                pt, pr = pT.next()

                def tr(e, pt=pt, q=q):
                    for j in range(8):
                        kc = q * 8 + j
                        ins = e.transpose(pt[:, j * 128:(j + 1) * 128], xn[:, kc * 128:(kc + 1) * 128], identb[:])
                    return ins
                A("pe", tr, reads=[xnr, "identb"], writes=[pr])
                for j in range(8):
                    kc = q * 8 + j
                    if q == 0:
                        A("act", lambda e, pt=pt, j=j, kc=kc: e.activation(out=dst(kc, col0, 128), in_=pt[:, j * 128:(j + 1) * 128], func=AF.Identity,
                                                                          bias=Scol[:, kc, r:r + 1], scale=Acol[:, kc, r:r + 1]),
                          reads=[pr] + MODC, writes=[(dst_res, kc)])
                    else:
                        A("dve", lambda e, pt=pt, j=j, kc=kc: e.tensor_scalar(out=dst(kc, col0, 128), in0=pt[:, j * 128:(j + 1) * 128],
                                                                             scalar1=Acol[:, kc, r:r + 1], scalar2=Scol[:, kc, r:r + 1],
                                                                             op0=ALU.mult, op1=ALU.add),
                          reads=[pr] + MODC, writes=[(dst_res, kc)])

        with ExitStack() as pf:
            hT_own = sbt(pf, "hT_own", [128, 16, NTOK], BF16)

            def hTs(kc, c0, n):
                if c0 < NTOK:
                    return hT_own[:, kc, c0:c0 + n]
                return shr[:, kc * 1024 + (c0 - NTOK):kc * 1024 + (c0 - NTOK) + n]
            with ExitStack() as pn:
                xt_ring = Ring("xt", [sbt(pn, "xt%d" % i, [128, D]) for i in range(2)])
                xn_ring = Ring("xn", [sbt(pn, "xn%d" % i, [128, D], BF16) for i in range(2)])
                prev = None
                for tt in range(20):
                    xt, xr = xt_ring.next()
                    A("sp", lambda e, xt=xt, tt=tt: e.dma_start(out=xt[:], in_=xrows(tt)), writes=[xr], key=xr)
                    cur = norm_stats(xt[:], xr, xn_ring) + (tt,)
                    if prev is not None:
                        norm_tr(prev[0], prev[1], A1, S1, 0 if prev[2] < 4 else 1, hTs, ("hT", prev[2]), prev[2] * 128)
                    prev = cur
                norm_tr(prev[0], prev[1], A1, S1, 1, hTs, ("hT", prev[2]), prev[2] * 128)
                S.barrier()
            chk(1)

            with ExitStack() as pa:
                cosT = sbt(pa, "cosT", [128, 2048], BF16); sinT = sbt(pa, "sinT", [128, 2048], BF16)
                A("pool", lambda e: e.dma_start(out=cosT[:], in_=c_cos), writes=["cosT"], key="cosT")
                A("pool", lambda e: e.dma_start(out=sinT[:], in_=c_sin), writes=["sinT"], key="sinT")
                wsm = Ring("wsm", [sbt(pa, "wsm%d" % i, [128, 16, 128], BF16) for i in range(3)])
                qT = sbt(pa, "qT", [128, NTOK], BF16)
                kT = [sbt(pa, "kT%d" % c, [128, NKEY], BF16) for c in range(2)]
                vv = sbt(pa, "vv", [128, 22, 130], BF16)
                ckb = sbt(pa, "ckb", [128, 2, 128], BF16); cvb = sbt(pa, "cvb", [128, 2, 128], BF16)
                sq_ring = Ring("sq", [sbt(pa, "sq%d" % i, [128, 512], BF16) for i in range(2)])
                rs_ring = Ring("rs", [sbt(pa, "rs%d" % i, [128, 512]) for i in range(2)])
                zn_ring = Ring("zn", [sbt(pa, "zn%d" % i, [128, 512], BF16) for i in range(2)])
                t1_ring = Ring("t1", [sbt(pa, "t1%d" % i, [128, 512]) for i in range(2)])
                t2_ring = Ring("t2", [sbt(pa, "t2%d" % i, [128, 512]) for i in range(2)])
                zf_ring = Ring("zf", [sbt(pa, "zf%d" % i, [128, 512]) for i in range(1)])
                vb_ring = Ring("vb", [sbt(pa, "vb%d" % i, [128, 512], BF16) for i in range(1)])
                so_ring = Ring("so", [sbt(pa, "so%d" % i, [128, 4, 128]) for i in range(1)])
                PT_ring = Ring("PT", [sbt(pa, "PT%d" % i, [128, 512], BF16) for i in range(4)])
                a0_t = sbt(pa, "a0_t", [128, 4, 128])
                o_raw = [sbt(pa, "o_raw%d" % c, [128, 4, 130]) for c in range(2)]
                o_t = sbt(pa, "o_t", [128, 4, 128])
                on_t = sbt(pa, "on_t", [128, 4, 128], BF16)
                A("dve", lambda e: e.memset(kT[0][64:128, :], 0.0), writes=["kTpad0"])
                A("dve", lambda e: e.memset(kT[1][0:64, :], 0.0), writes=["kTpad1"])
                A("dve", lambda e: e.memset(vv[:, :, 128:130], 1.0), writes=["vones"])
                wa2 = Ring("wa2", [sbt(pa, "wa2%d" % i, [128, 16, 256], BF16) for i in range(2)])
                mch = sbt(pa, "mch", [2, D])

                ada_blocks = [(chunk, blk) for chunk in (3, 4, 2, 5) for blk in range(8)]
                ada_state = {"k": 0, "slots": {}}

                def ada_dma(k):
                    chunk, blk = ada_blocks[k]
                    wt, wr = wa2.next()
                    c0 = chunk * D + blk * 256
                    A("pool", lambda e: e.dma_start(out=wt[:], in_=w_ada_v[:, :, c0:c0 + 256]), writes=[wr], key=wr)
                    ada_state["slots"][k] = (wt, wr)

                def ada_finish(chunk):
                    MCH = [("mch", b) for b in range(8)]
                    if chunk in (3, 4):
                        dst = S2 if chunk == 3 else A2
                        pt, pr = pS.next()

                        def tr(e):
                            for kc in range(16):
                                ins = e.transpose(pt[:, kc * 2:kc * 2 + 2], mch[0:2, kc * 128:(kc + 1) * 128], identf[0:2, 0:2])
                            return ins
                        A("pe", tr, reads=MCH + ["identf"], writes=[pr])
                        A("dve", lambda e: e.tensor_copy(out=dst[:].rearrange("p k r -> p (k r)"), in_=pt[:, 0:32]), reads=[pr], writes=[("modc2", chunk)])
                        if chunk == 4:
                            for r in range(2):
                                A("dve", lambda e, r=r: e.scalar_tensor_tensor(out=A2[:, :, r], in0=A2[:, :, r], scalar=1.0, in1=nw2_t[:], op0=ALU.add, op1=ALU.mult),
                                  reads=[("modc2", 4), "nw2_t"], writes=[("modc2", 4)])
                    else:
                        gi = 0 if chunk == 2 else 1
                        A("sp", lambda e: e.dma_start(out=gsc[2 * gi:2 * gi + 2, :], in_=mch[:]), reads=MCH, key=("gsc", gi))

                def ada_next(n=1):
                    for _ in range(n):
                        k = ada_state["k"]
                        if k >= len(ada_blocks):
                            return
                        ada_state["k"] = k + 1
                        if k == 0:
                            ada_dma(0)
                        if k + 1 < len(ada_blocks):
                            ada_dma(k + 1)
                        chunk, blk = ada_blocks[k]
                        if blk == 0:
                            A("sp", lambda e: e.dma_start(out=mch[:], in_=bada2[:, chunk * D:(chunk + 1) * D]),
                              writes=[("mch", b) for b in range(8)], key="mch")
                        wt, wr = ada_state["slots"].pop(k)
                        pt, pr = pg.tiles[1], ("pg", 1)

                        def mm(e):
                            for kc in range(16):
                                ins = e.matmul(pt[0:2, 0:256], lhsT=scT[:, kc, :], rhs=wt[:, kc, :], start=(kc == 0), stop=(kc == 15))
                            return ins
                        A("pe", mm, reads=[wr, "scT"], writes=[pr])
                        A("dve", lambda e: e.tensor_tensor(out=mch[0:2, blk * 256:(blk + 1) * 256], in0=pt[0:2, 0:256],
                                                           in1=mch[0:2, blk * 256:(blk + 1) * 256], op=ALU.add),
                          reads=[pr, ("mch", blk)], writes=[("mch", blk)])
                        if blk == 7:
                            ada_finish(chunk)

                pq = Ring("pq", [pg.tiles[0], pg.tiles[1], pO[0], pO[1]])
                pq_names = [("pg", 0), ("pg", 1), ("pO", 0), ("pO", 1)]

                def pq_next():
                    j = pq.i % 4
                    pq.i += 1
                    return pq.tiles[j], pq_names[j]

                def proj_block(wt, wr, tb, ring4=False):
                    pt, pr = pq_next() if ring4 else pg.next()

                    def mm(e):
                        for kc in range(16):
                            ins = e.matmul(pt[:], lhsT=wt[:, kc, :], rhs=hTs(kc, tb * 512, 512), start=(kc == 0), stop=(kc == 15))
                        return ins
                    A("pe", mm, reads=[wr] + [("hT", tb * 4 + j) for j in range(4)], writes=[pr])
                    return pt, pr

                def state_out(zf, zfr, dram, h):
                    pt, pr = pS.next()

                    def tr(e):
                        for j in range(4):
                            ins = e.transpose(pt[:, j * 128:(j + 1) * 128], zf[:, j * 128:(j + 1) * 128], identf[:])
                        return ins
                    A("pe", tr, reads=[zfr, "identf"], writes=[pr])
                    so, sor = so_ring.next()
                    A("act", lambda e: e.activation(out=so[:].rearrange("p a b -> p (a b)"), in_=pt[:], func=AF.Copy), reads=[pr], writes=[sor])
                    A("sp", lambda e: e.dma_start(out=dram[:, h * 128:(h + 1) * 128].rearrange("(t p) n -> p t n", p=128), in_=so[:]),
                      reads=[sor], key=sor)

                def stA(st):
                    st["pt"], st["pr"] = proj_block(st["wt"], st["wr"], st["tb"], ring4=True)
                    st["sq"], st["sqr"] = sq_ring.next()
                    A("act", lambda e: e.activation(out=st["sq"][:], in_=st["pt"][:], func=AF.Square), reads=[st["pr"]], writes=[st["sqr"]])

                def stB(st, h):
                    pt, pr, sq, sqr, tb, kind = st["pt"], st["pr"], st["sq"], st["sqr"], st["tb"], st["kind"]
                    wcol = 0 if kind == "q" else 1
                    c0 = tb * 512
                    p2, p2r = pS.next()
                    A("pe", lambda e: e.matmul(p2[:], lhsT=bones[:], rhs=sq[:], start=True, stop=True), reads=[sqr, "bones"], writes=[p2r])
                    rs, rsr = rs_ring.next()
                    A("act", lambda e: e.activation(out=rs[:], in_=p2[:], func=AF.Sqrt, bias=epsc[:, 0:1], scale=1.0), reads=[p2r, "epsc"], writes=[rsr])
                    A("dve", lambda e: e.reciprocal(out=rs[:], in_=rs[:]), reads=[rsr], writes=[rsr])
                    if tb == 0:
                        zf, zfr = zf_ring.next()
                        A("dve", lambda e: e.scalar_tensor_tensor(out=zf[:], in0=pt[:], scalar=qkw_t[:, wcol:wcol + 1], in1=rs[:], op0=ALU.mult, op1=ALU.mult),
                          reads=[pr, rsr, "qkw"], writes=[zfr])
                        if kind == "q":
                            A("act", lambda e: e.activation(out=qT[:, c0:c0 + 512], in_=zf[:], func=AF.Copy), reads=[zfr], writes=[("qT", tb)])
                        else:
                            A("dve", lambda e: e.tensor_copy(out=kT[0][0:64, c0:c0 + 512], in_=zf[0:64, :]), reads=[zfr], writes=[("kT0", tb)])
                            A("act", lambda e: e.activation(out=kT[1][64:128, c0:c0 + 512], in_=zf[64:128, :], func=AF.Copy), reads=[zfr], writes=[("kT1", tb)])
                        st["zf"], st["zfr"] = zf, zfr
                    else:
                        zn, znr = zn_ring.next()
                        A("dve", lambda e: e.scalar_tensor_tensor(out=zn[:], in0=pt[:], scalar=qkw_t[:, wcol:wcol + 1], in1=rs[:], op0=ALU.mult, op1=ALU.mult),
                          reads=[pr, rsr, "qkw"], writes=[znr])
                        st["zn"], st["znr"] = zn, znr

                def stC(st, h):
                    tb, kind = st["tb"], st["kind"]
                    c0 = tb * 512
                    if tb == 0:
                        if kind == "k":
                            state_out(st["zf"], st["zfr"], sk, h)
                        return
                    zn, znr = st["zn"], st["znr"]
                    p3, p3r = pS.next()
                    A("pe", lambda e: e.matmul(p3[:], lhsT=perm[:], rhs=zn[:], start=True, stop=True), reads=[znr, "perm"], writes=[p3r])
                    r0 = (tb - 1) * 512
                    t1, t1r = t1_ring.next()
                    A("pool", lambda e: e.tensor_tensor(out=t1[:], in0=zn[:], in1=cosT[:, r0:r0 + 512], op=ALU.mult), reads=[znr, "cosT"], writes=[t1r])
                    t2, t2r = t2_ring.next()
                    A("dve", lambda e: e.tensor_tensor(out=t2[:], in0=p3[:], in1=sinT[:, r0:r0 + 512], op=ALU.mult), reads=[p3r, "sinT"], writes=[t2r])
                    if kind == "q":
                        A("pool", lambda e: e.tensor_tensor(out=qT[:, c0:c0 + 512], in0=t1[:], in1=t2[:], op=ALU.add), reads=[t1r, t2r], writes=[("qT", tb)])
                    else:
                        A("pool", lambda e: e.tensor_tensor(out=kT[0][0:64, c0:c0 + 512], in0=t1[0:64, :], in1=t2[0:64, :], op=ALU.add),
                          reads=[t1r, t2r], writes=[("kT0", tb)])
                        A("dve", lambda e: e.tensor_tensor(out=kT[1][64:128, c0:c0 + 512], in0=t1[64:128, :], in1=t2[64:128, :], op=ALU.add),
                          reads=[t1r, t2r], writes=[("kT1", tb)])

                s4_tiles = [pS.tiles[0], pS.tiles[1], pg.tiles[0]]
                s4_names = [("pS", 0), ("pS", 1), ("pg", 0)]
                s4_i = [0]

                def s4_next():
                    j = s4_i[0] % 3
                    s4_i[0] += 1
                    return s4_tiles[j], s4_names[j]

                def attention(h, q0, nq, groups, prev):
                    nqt = nq // 128
                    nbk = (nqt + 1) // 2
                    pO_res = [("pO", 0), ("pO", 1)]
                    single = len(groups) == 1
                    maxlen = max(len(g[2]) for g in groups)
                    for c in range(2):
                        for b in range(nbk):
                            A("pe", lambda e, b=b: e.matmul(pO[b][:, 0:260], lhsT=zerob[:, 0:128], rhs=zerob[:, 0:260], start=True, stop=True,
                                                           skip_group_check=True),
                              reads=["zerob"], writes=[pO_res[b]])

                        def s_mm(ki, c=c):
                            ps_, psr = s4_next()
                            act = [g for g in groups if ki < len(g[2])]

                            def fn(e):
                                for g in act:
                                    kt = g[2][ki]
                                    a_ = g[0] * 128
                                    n_ = g[1] * 128
                                    ins = e.matmul(ps_[:, a_:a_ + n_], lhsT=kT[c][:, kt * 128:(kt + 1) * 128], rhs=qT[:, q0 + a_:q0 + a_ + n_],
                                                   start=True, stop=True)
                                return ins
                            A("pe", fn, reads=[("kT%d" % c, g[2][ki] // 4) for g in act] + ["kTpad%d" % c, ("qT", q0 // 512)], writes=[psr])
                            return ps_, psr
                        nxt = [s_mm(d_) for d_ in range(min(2, maxlen))]
                        for ki in range(maxlen):
                            ps_, psr = nxt.pop(0)
                            if ki + 2 < maxlen:
                                nxt.append(s_mm(ki + 2))
                            P, Pr = PT_ring.next()
                            A("act", lambda e, ps_=ps_, P=P: e.activation(out=P[:, 0:nq], in_=ps_[:, 0:nq], func=AF.Exp, scale=0.125), reads=[psr], writes=[Pr])
                            act = [g for g in groups if ki < len(g[2])]

                            def av(e, P=P, act=act):
                                for g in act:
                                    kt = g[2][ki]
                                    for j in range(g[1]):
                                        qi = g[0] + j
                                        bank = pO[qi // 2]; off = (qi % 2) * 130
                                        ins = e.matmul(bank[:, off:off + 129], lhsT=P[:, qi * 128:(qi + 1) * 128], rhs=vv[:, kt, 0:129],
                                                       start=False, stop=(ki == len(g[2]) - 1), skip_group_check=True)
                                return ins
                            A("pe", av, reads=[Pr, "vones"] + [("vv", g[2][ki]) for g in act], writes=pO_res[0:nbk])
                            if ki % 9 == 6:
                                ada_next(1)
                            if c == 0 and prev is not None and ki == min(3, maxlen - 1):
                                prev[0]()
                        if c == 0 and prev is not None:
                            prev[1]()
                        for b in range(nbk):
                            A("dve", lambda e, b=b, c=c: e.tensor_copy(out=o_raw[c][:, 2 * b:2 * b + 2, :].rearrange("p a b -> p (a b)"), in_=pO[b][:, 0:260]),
                              reads=[pO_res[b]], writes=[("oraw", c, b)])
                    ORAW = [("oraw", c, b) for c in range(2) for b in range(nbk)]
                    sb_, sbr = st_b.next()
                    for c in range(2):
                        A("dve", lambda e, c=c: e.reciprocal(out=sb_[:, 4 * c:4 * c + nqt], in_=o_raw[c][:, 0:nqt, 128]), reads=ORAW, writes=[(sbr, c)])
                    A("dve", lambda e: e.tensor_scalar(out=sb_[:, 4:4 + nqt], in0=sb_[:, 4:4 + nqt], scalar1=neglam[:, 0:1], scalar2=None, op0=ALU.mult),
                      reads=[(sbr, 1), "neglam"], writes=[(sbr, 1)])
                    for qi in range(nqt):
                        A("dve", lambda e, qi=qi: e.tensor_scalar(out=a0_t[:, qi, :], in0=o_raw[0][:, qi, 0:128], scalar1=sb_[:, qi:qi + 1], scalar2=None, op0=ALU.mult),
                          reads=ORAW + [(sbr, 0)], writes=[("a0", qi)])
                        A("dve", lambda e, qi=qi: e.scalar_tensor_tensor(out=o_t[:, qi, :], in0=o_raw[1][:, qi, 0:128], scalar=sb_[:, 4 + qi:5 + qi], in1=a0_t[:, qi, :],
                                                                        op0=ALU.mult, op1=ALU.add),
                          reads=ORAW + [(sbr, 1), ("a0", qi)], writes=[("o_t", qi)])
                    OT = [("o_t", qi) for qi in range(nqt)]
                    A("dve", lambda e: e.tensor_tensor(out=a0_t[:, 0:nqt, :], in0=o_t[:, 0:nqt, :], in1=o_t[:, 0:nqt, :], op=ALU.mult),
                      reads=OT, writes=[("a0", qi) for qi in range(nqt)])
                    sc_, scr = st_a.next()
                    A("dve", lambda e: e.reduce_sum(out=sc_[:, 0:nqt], in_=a0_t[:, 0:nqt, :], axis=mybir.AxisListType.X),
                      reads=[("a0", qi) for qi in range(nqt)], writes=[scr])

                    def fin2():
                        A("act", lambda e: e.activation(out=sc_[:, 4:4 + nqt], in_=sc_[:, 0:nqt], func=AF.Ln, bias=epsc[:, 0:1], scale=1.0 / 128), reads=[scr, "epsc"], writes=[scr])
                        A("act", lambda e: e.activation(out=sc_[:, 4:4 + nqt], in_=sc_[:, 4:4 + nqt], func=AF.Exp, scale=-0.5), reads=[scr], writes=[scr])
                        for qi in range(nqt):
                            A("dve", lambda e, qi=qi: e.scalar_tensor_tensor(out=on_t[:, qi, :], in0=o_t[:, qi, :], scalar=sc_[:, 4 + qi:5 + qi], in1=subw[:], op0=ALU.mult, op1=ALU.mult),
                              reads=[("o_t", qi), scr, "subw"], writes=[("on_t", qi)])

                    def tail():
                        pt, pr = pT.next()

                        def tr(e):
                            for qi in range(nqt):
                                ins = e.transpose(pt[:, qi * 128:(qi + 1) * 128], on_t[:, qi, :], identb[:])
                            return ins
                        A("pe", tr, reads=[("on_t", qi) for qi in range(nqt)] + ["identb"], writes=[pr])
                        A("act", lambda e: e.activation(out=attT[:, h, q0:q0 + nq], in_=pt[:, 0:nq], func=AF.Copy),
                          reads=[pr], writes=[("attT", h, q0)])
                    return fin2, tail

                def v_block(h, wv, wvr, tb):
                    pt, pr = proj_block(wv, wvr, tb, ring4=True)
                    vb, vbr = vb_ring.next()
                    A("act", lambda e: e.activation(out=vb[:], in_=pt[:], func=AF.Copy), reads=[pr], writes=[vbr])
                    if tb == 0:
                        p4, p4r = pS.next()

                        def mmv(e):
                            for j in range(4):
                                for kc in range(16):
                                    ins = e.matmul(p4[:, j * 128:(j + 1) * 128], lhsT=hTs(kc, j * 128, 128), rhs=wv[:, kc, :], start=(kc == 0), stop=(kc == 15))
                            return ins
                        A("pe", mmv, reads=[wvr] + [("hT", j) for j in range(4)], writes=[p4r])
                        so, sor = so_ring.next()
                        A("act", lambda e: e.activation(out=so[:].rearrange("p a b -> p (a b)"), in_=p4[:], func=AF.Copy), reads=[p4r], writes=[sor])
                        A("sp", lambda e: e.dma_start(out=sv[:, h * 128:(h + 1) * 128].rearrange("(t p) n -> p t n", p=128), in_=so[:]),
                          reads=[sor], key=sor)
                    ptb, ptbr = pT.next()

                    def trv(e):
                        for j in range(4):
                            ins = e.transpose(ptb[:, j * 128:(j + 1) * 128], vb[:, j * 128:(j + 1) * 128], identb[:])
                        return ins
                    A("pe", trv, reads=[vbr, "identb"], writes=[ptbr])
                    for j in range(4):
                        A("dve", lambda e, j=j: e.tensor_copy(out=vv[:, tb * 4 + j, 0:128], in_=ptb[:, j * 128:(j + 1) * 128]),
                          reads=[ptbr], writes=[("vv", tb * 4 + j)])

                prev = None
                for h in range(8):
                    wq, wqr = wsm.next()
                    A("pool", lambda e, wq=wq, h=h: e.dma_start(out=wq[:], in_=w_in_v[:, :, h * 128:(h + 1) * 128]), writes=[wqr], key=wqr)
                    wk, wkr = wsm.next()
                    A("pool", lambda e, wk=wk, h=h: e.dma_start(out=wk[:], in_=w_in_v[:, :, 1024 + h * 128:1024 + (h + 1) * 128]), writes=[wkr], key=wkr)
                    wv, wvr = wsm.next()
                    A("pool", lambda e, wv=wv, h=h: e.dma_start(out=wv[:], in_=w_in_v[:, :, 2048 + h * 128:2048 + (h + 1) * 128]), writes=[wvr], key=wvr)
                    A("pool", lambda e, h=h: e.dma_start(out=ckb[:], in_=ck[:, h * 128:(h + 1) * 128].rearrange("(t p) n -> p t n", p=128)), writes=["ckb"], key="ckb")
                    A("pool", lambda e, h=h: e.dma_start(out=cvb[:], in_=cv[:, h * 128:(h + 1) * 128].rearrange("(t p) n -> p t n", p=128)), writes=["cvb"], key="cvb")
                    blocks = ([dict(kind="k", tb=tb, wt=wk, wr=wkr) for tb in (1, 2, 3, 4)] + [dict(kind="q", tb=tb, wt=wq, wr=wqr) for tb in (1, 2)]
                              + [dict(kind="q", tb=0, wt=wq, wr=wqr), dict(kind="k", tb=0, wt=wk, wr=wkr)])
                    vorder = [1, 2, 3, 4, 0]
                    nb_ = len(blocks)
                    vdone = 0
                    for step in range(nb_ + 2):
                        if step < nb_:
                            stA(blocks[step])
                        if step == 1 and prev is not None:
                            prev[0]()
                            prev[1]()
                            prev = None
                        if 0 <= step - 1 < nb_:
                            stB(blocks[step - 1], h)
                        if 0 <= step - 2 < nb_:
                            stC(blocks[step - 2], h)
                        if step >= 2 and vdone < 5:
                            v_block(h, wv, wvr, vorder[vdone])
                            vdone += 1
                    while vdone < 5:
                        v_block(h, wv, wvr, vorder[vdone])
                        vdone += 1
                    ptb, ptbr = pT.next()

                    def trc(e, ptb=ptb):
                        for j in range(2):
                            ins = e.transpose(ptb[:, j * 128:(j + 1) * 128], ckb[:, j, :], identb[:])
                        return ins
                    A("pe", trc, reads=["ckb", "identb"], writes=[ptbr])
                    A("dve", lambda e, ptb=ptb: e.tensor_copy(out=kT[0][0:64, NALL:NKEY], in_=ptb[0:64, 0:256]), reads=[ptbr], writes=[("kT0", 5)])
                    A("dve", lambda e, ptb=ptb: e.tensor_copy(out=kT[1][64:128, NALL:NKEY], in_=ptb[64:128, 0:256]), reads=[ptbr], writes=[("kT1", 5)])
                    for j in range(2):
                        A("dve", lambda e, j=j: e.tensor_copy(out=vv[:, 20 + j, 0:128], in_=cvb[:, j, :]), reads=["cvb"], writes=[("vv", 20 + j)])
                    prev = attention(h, 512, 512, [(0, 4, list(range(4, 22)))], prev)
                    prev = attention(h, 1024, 512, [(0, 4, list(range(4, 22)))], prev)
                    prev = attention(h, 0, 512, [(0, 2, [0, 1]), (2, 2, [2, 3])], prev)
                    prev[0]()
                    prev = ((lambda: None), prev[1])
                    if h == 0:
                        chk(2)
                prev[0]()
                prev[1]()
                ada_next(100)
                S.barrier()
            chk(3)

            with ExitStack() as pg_:
                wg = sbt(pg_, "wg", [128, 16, 1024], BF16)
                gn = sbt(pg_, "gn", [128, 12, 1024], BF16)
                sgw = sbt(pg_, "sgw", [128, 1024]); wsb = sbt(pg_, "wsb", [128, 1024], BF16); bsf = sbt(pg_, "bsf", [1, 1024]); onesf = sbt(pg_, "onesf", [1, 128])
                gg_ring = Ring("gg", [sbt(pg_, "gg%d" % i, [128, 1024]) for i in range(1)])
                gu_ring = Ring("gu", [sbt(pg_, "gu%d" % i, [128, 512], BF16) for i in range(2)])
                wu_ring = Ring("wu", [sbt(pg_, "wu%d" % i, [128, 16, 128], BF16) for i in range(2)])
                for nq_ in range(4):
                    A("pool", lambda e, nq_=nq_: e.dma_start(out=wg[:, :, nq_ * 256:(nq_ + 1) * 256], in_=w_in_v[:, :, 4096 + nq_ * 256:4096 + (nq_ + 1) * 256]),
                      writes=[("wg", nq_)], key=("wg", nq_))
                A("sp", lambda e: e.dma_start(out=sgw[:], in_=sgunw), writes=["sgw"], key="sgw")
                A("pool", lambda e: e.dma_start(out=wsb[:], in_=wsT), writes=["wsb"], key="wsb")
                A("sp", lambda e: e.dma_start(out=bsf[:], in_=bsr), writes=["bsf"], key="bsf")
                A("dve", lambda e: e.memset(onesf[:], 1.0), writes=["onesf"])
                for tt in range(12):
                    gg, ggr = gg_ring.next()
                    sa, sar = st_a.next()
                    for nb in range(2):
                        pt, pr = p6_next()

                        def mm(e, pt=pt, nb=nb, tt=tt):
                            for kc in range(16):
                                ins = e.matmul(pt[:], lhsT=hTs(kc, tt * 128, 128), rhs=wg[:, kc, nb * 512:(nb + 1) * 512], start=(kc == 0), stop=(kc == 15))
                            return ins
                        A("pe", mm, reads=[("wg", 2 * nb), ("wg", 2 * nb + 1), ("hT", tt)], writes=[pr])
                        A("act", lambda e, pt=pt, nb=nb, gg=gg, sa=sa: e.activation(out=gg[:, nb * 512:(nb + 1) * 512], in_=pt[:], func=AF.Gelu_apprx_tanh,
                                                                                   accum_out=sa[:, nb:nb + 1]),
                          reads=[pr], writes=[(ggr, nb), (sar, nb)])
                    gjr = ("gn", tt)
                    A("act", lambda e, gg=gg, sa=sa, tt=tt: e.activation(out=gn[:, tt, :], in_=gg[:], func=AF.Square, accum_out=sa[:, 2:3]),
                      reads=[(ggr, 0), (ggr, 1)], writes=[gjr, (sar, 2)])
                    A("dve", lambda e, sa=sa: e.tensor_tensor(out=sa[:, 3:4], in0=sa[:, 0:1], in1=sa[:, 1:2], op=ALU.add), reads=[(sar, 0), (sar, 1)], writes=[(sar, 3)])
                    A("dve", lambda e, sa=sa: e.tensor_scalar(out=sa[:, 3:4], in0=sa[:, 3:4], scalar1=1.0 / 1024, scalar2=None, op0=ALU.mult), reads=[(sar, 3)], writes=[(sar, 3)])
                    A("dve", lambda e, sa=sa: e.tensor_tensor(out=sa[:, 4:5], in0=sa[:, 3:4], in1=sa[:, 3:4], op=ALU.mult), reads=[(sar, 3)], writes=[(sar, 4)])
                    A("dve", lambda e, sa=sa: e.scalar_tensor_tensor(out=sa[:, 5:6], in0=sa[:, 2:3], scalar=1.0 / 1024, in1=sa[:, 4:5], op0=ALU.mult, op1=ALU.subtract),
                      reads=[(sar, 2), (sar, 4)], writes=[(sar, 5)])
                    A("act", lambda e, sa=sa: e.activation(out=sa[:, 6:7], in_=sa[:, 5:6], func=AF.Sqrt, bias=epsc[:, 0:1], scale=1.0), reads=[(sar, 5), "epsc"], writes=[(sar, 6)])
                    A("dve", lambda e, sa=sa: e.reciprocal(out=sa[:, 6:7], in_=sa[:, 6:7]), reads=[(sar, 6)], writes=[(sar, 6)])
                    A("dve", lambda e, sa=sa: e.scalar_tensor_tensor(out=sa[:, 7:8], in0=sa[:, 3:4], scalar=-1.0, in1=sa[:, 6:7], op0=ALU.mult, op1=ALU.mult),
                      reads=[(sar, 3), (sar, 6)], writes=[(sar, 7)])
                    A("act", lambda e, gg=gg, sa=sa: e.activation(out=gg[:], in_=gg[:], func=AF.Identity, bias=sa[:, 7:8], scale=sa[:, 6:7]),
                      reads=[(ggr, 0), (ggr, 1), (sar, 6), (sar, 7), gjr], writes=[(ggr, 0), (ggr, 1)])
                    A("dve", lambda e, gg=gg, tt=tt: e.tensor_tensor(out=gn[:, tt, :], in0=gg[:], in1=sgw[:], op=ALU.mult), reads=[(ggr, 0), (ggr, 1), "sgw"], writes=[("gn", tt)])
                for gi in range(8):
                    wu, wur = wu_ring.next()
                    A("pool", lambda e, wu=wu, gi=gi: e.dma_start(out=wu[:], in_=w_in_v[:, :, 3072 + gi * 128:3072 + (gi + 1) * 128]), writes=[wur], key=wur)
                    for tb in range(3):
                        pt, pr = p6_next()

                        def mm(e, pt=pt, wu=wu, tb=tb):
                            for kc in range(16):
                                ins = e.matmul(pt[:], lhsT=wu[:, kc, :], rhs=hTs(kc, tb * 512, 512), start=(kc == 0), stop=(kc == 15))
                            return ins
                        A("pe", mm, reads=[wur] + [("hT", tb * 4 + j) for j in range(4)], writes=[pr])
                        gu, gur = gu_ring.next()
                        A("act", lambda e, pt=pt, gu=gu: e.activation(out=gu[:], in_=pt[:], func=AF.Gelu_apprx_tanh), reads=[pr], writes=[gur])
                        pm, pmr = p6_next()

                        def mix(e, pm=pm, gi=gi, tb=tb):
                            for j in range(4):
                                tt = tb * 4 + j
                                e.matmul(pm[:, j * 128:(j + 1) * 128], lhsT=gn[:, tt, gi * 128:(gi + 1) * 128], rhs=wsb[:, gi * 128:(gi + 1) * 128], start=True, stop=False)
                                ins = e.matmul(pm[:, j * 128:(j + 1) * 128], lhsT=onesf[0:1, :], rhs=bsf[0:1, gi * 128:(gi + 1) * 128], start=False, stop=True)
                            return ins
                        A("pe", mix, reads=[("gn", tb * 4 + j) for j in range(4)] + ["wsb", "bsf", "onesf"], writes=[pmr])
                        A("dve", lambda e, pm=pm, gu=gu, gi=gi, tb=tb: e.tensor_tensor(out=sguT_ap(gi, tb * 512, 512), in0=pm[:], in1=gu[:], op=ALU.mult),
                          reads=[pmr, gur], writes=[("sguT", gi, tb)])
                S.barrier()
            chk(4)

        with ExitStack() as pb:
            acc = sbt(pb, "acc", [128, 6, D])
            h2T = sbt(pb, "h2T", [128, 16, HALF], BF16)
            aT = sbt(pb, "aT", [128, 16, HALF], BF16)
            wb = Ring("wb", [sbt(pb, "wb%d" % i, [128, 16, 256], BF16) for i in range(2)])
            xst_ring = Ring("xst", [sbt(pb, "xst%d" % i, [128, 256]) for i in range(3)])
            tmp_ring = Ring("tmp", [sbt(pb, "tmp%d" % i, [128, 256]) for i in range(3)])
            rl_ring = Ring("rl", [sbt(pb, "rl%d" % i, [128, 512], BF16) for i in range(2)])
            xn2_ring = Ring("xn2", [sbt(pb, "xn2%d" % i, [128, D], BF16) for i in range(2)])
            GATES = [("gates", gi, r, nb) for gi in range(2) for r in range(2) for nb in range(4)]
            gates = sbt(pb, "gates", [128, 4, D], BF16)
            for gi in range(2):
                A("sp", lambda e, gi=gi: e.dma_start(out=acc[0:2, gi, :], in_=gsc[2 * gi:2 * gi + 2, :]), writes=[("gstage", gi)], key=("gstage", gi))
                for r in range(2):
                    for nb in range(4):
                        pt, pr = p6_next()
                        A("pe", lambda e, pt=pt, r=r, gi=gi, nb=nb: e.matmul(pt[:], lhsT=sel[0:2, r * 128:(r + 1) * 128], rhs=acc[0:2, gi, nb * 512:(nb + 1) * 512],
                                                                          start=True, stop=True),
                          reads=["sel", ("gstage", gi)], writes=[pr])
                        A("act", lambda e, pt=pt, gi=gi, r=r, nb=nb: e.activation(out=gates[:, gi * 2 + r, nb * 512:(nb + 1) * 512], in_=pt[:], func=AF.Copy),
                          reads=[pr], writes=[("gates", gi, r, nb)])
            S.barrier()
            GATES = []
            for half in range(2):
                t0 = half * 6
                for nb in range(8):
                    wt, wr = wb.next()
                    A("pool", lambda e, wt=wt, nb=nb: e.dma_start(out=wt[:], in_=w_o_v[:, :, nb * 256:(nb + 1) * 256]), writes=[wr], key=wr)
                    for ti in range(6):
                        tt = t0 + ti
                        r = 0 if tt < 4 else 1
                        pt, pr = p6_next()

                        def mm(e, pt=pt, wt=wt, tt=tt):
                            for kc in range(16):
                                lt = attT[:, kc, tt * 128:(tt + 1) * 128] if kc < 8 else sguT_ap(kc - 8, tt * 128, 128)
                                ins = e.matmul(pt[:, 0:256], lhsT=lt, rhs=wt[:, kc, :], start=(kc == 0), stop=(kc == 15))
                            return ins
                        A("pe", mm, reads=[wr] + [("sguT", g_, tt // 4) for g_ in range(8)], writes=[pr])
                        xst, xsr = xst_ring.next()
                        A("sp", lambda e, xst=xst, tt=tt, nb=nb: e.dma_start(out=xst[:], in_=xrows(tt)[:, nb * 256:(nb + 1) * 256]), writes=[xsr], key=xsr)
                        tmp, tmr = tmp_ring.next()
                        A("dve", lambda e, pt=pt, tmp=tmp, r=r, nb=nb: e.tensor_tensor(out=tmp[:], in0=pt[:, 0:256], in1=gates[:, r, nb * 256:(nb + 1) * 256], op=ALU.mult),
                          reads=[pr] + GATES, writes=[tmr])
                        A("dve", lambda e, tmp=tmp, xst=xst, ti=ti, nb=nb: e.tensor_tensor(out=acc[:, ti, nb * 256:(nb + 1) * 256], in0=tmp[:], in1=xst[:], op=ALU.add),
                          reads=[tmr, xsr], writes=[("acc", ti)])
                h2d = (lambda kc, c0, n: h2T[:, kc, c0:c0 + n])
                prev = None
                for ti in range(6):
                    cur = norm_stats(acc[:, ti, :], ("acc", ti), xn2_ring) + (ti,)
                    if prev is not None:
                        norm_tr(prev[0], prev[1], A2, S2, 0 if t0 + prev[2] < 4 else 1, h2d, ("h2T", prev[2]), prev[2] * 128)
                    prev = cur
                norm_tr(prev[0], prev[1], A2, S2, 0 if t0 + prev[2] < 4 else 1, h2d, ("h2T", prev[2]), prev[2] * 128)
                for rd in range(4):
                    for fb in range(8):
                        wt, wr = wb.next()
                        f0 = rd * 2048 + fb * 256
                        A("pool", lambda e, wt=wt, f0=f0: e.dma_start(out=wt[:], in_=w_ff1_v[:, :, f0:f0 + 256]), writes=[wr], key=wr)
                        for fj in range(2):
                            fc = fb * 2 + fj
                            for (c0, cn) in ((0, 512), (512, 256)):
                                pt, pr = p6_next()

                                def mm(e, pt=pt, wt=wt, fj=fj, c0=c0, cn=cn):
                                    for kc in range(16):
                                        ins = e.matmul(pt[:, 0:cn], lhsT=wt[:, kc, fj * 128:(fj + 1) * 128], rhs=h2T[:, kc, c0:c0 + cn], start=(kc == 0), stop=(kc == 15))
                                    return ins
                                A("pe", mm, reads=[wr] + [(("h2T", ti), kc) for ti in range(c0 // 128, (c0 + cn) // 128) for kc in range(16)], writes=[pr])
                                rl, rlr = rl_ring.next()
                                A("act", lambda e, pt=pt, rl=rl, cn=cn: e.activation(out=rl[:, 0:cn], in_=pt[:, 0:cn], func=AF.Relu), reads=[pr], writes=[rlr])
                                A("dve", lambda e, rl=rl, fc=fc, c0=c0, cn=cn: e.tensor_tensor(out=aT[:, fc, c0:c0 + cn], in0=rl[:, 0:cn], in1=rl[:, 0:cn], op=ALU.mult),
                                  reads=[rlr], writes=[("aT", fc, c0)])
                    for nb in range(8):
                        wt, wr = wb.next()
                        A("pool", lambda e, wt=wt, rd=rd, nb=nb: e.dma_start(out=wt[:], in_=w_ff2_v[:, rd * 16:(rd + 1) * 16, nb * 256:(nb + 1) * 256]), writes=[wr], key=wr)
                        for ti in range(6):
                            tt = t0 + ti
                            r = 0 if tt < 4 else 1
                            pt, pr = p6_next()

                            def mm(e, pt=pt, wt=wt, ti=ti):
                                for fc in range(16):
                                    ins = e.matmul(pt[:, 0:256], lhsT=aT[:, fc, ti * 128:(ti + 1) * 128], rhs=wt[:, fc, :], start=(fc == 0), stop=(fc == 15))
                                return ins
                            A("pe", mm, reads=[wr] + [("aT", fc, 0 if ti < 4 else 512) for fc in range(16)], writes=[pr])
                            tmp, tmr = tmp_ring.next()
                            A("dve", lambda e, pt=pt, tmp=tmp, r=r, nb=nb: e.tensor_tensor(out=tmp[:], in0=pt[:, 0:256], in1=gates[:, 2 + r, nb * 256:(nb + 1) * 256], op=ALU.mult),
                              reads=[pr] + GATES, writes=[tmr])
                            A("dve", lambda e, tmp=tmp, ti=ti, nb=nb: e.tensor_tensor(out=acc[:, ti, nb * 256:(nb + 1) * 256], in0=acc[:, ti, nb * 256:(nb + 1) * 256], in1=tmp[:], op=ALU.add),
                              reads=[tmr, ("acc", ti)], writes=[("acc", ti)])
                for ti in range(6):
                    tt = t0 + ti
                    A("sp", lambda e, ti=ti, tt=tt: e.dma_start(out=yrows(tt), in_=acc[:, ti, :]), reads=[("acc", ti)], key=("yout", ti))
                chk(5 + half)
            S.barrier(engines=("sp",))
    return nc


def _rope_tables(order):
    n_freq = 16
    freqs = (10000.0 ** (-np.arange(n_freq, dtype=np.float32) / n_freq)).astype(np.float32)
    r = (order // 64).astype(np.float32)
    col = (order % 64).astype(np.float32)
    ang_r = r[:, None] * freqs
    ang_c = col[:, None] * freqs
    ang = np.concatenate([ang_r, ang_r, ang_c, ang_c], axis=-1)
    cos = np.cos(ang).astype(np.float32).T
    sin = np.sin(ang).astype(np.float32).T
    return np.ascontiguousarray(np.concatenate([cos, cos], 0)), np.ascontiguousarray(np.concatenate([sin, sin], 0))


def _consts():
    ident = np.eye(128, dtype=np.float32)
    bones = np.zeros((128, 128), np.float32)
    bones[0:64, 0:64] = 1.0 / 64
    bones[64:128, 64:128] = 1.0 / 64
    perm = np.zeros((128, 128), np.float32)
    for d in range(128):
        if d % 32 < 16:
            perm[d + 16, d] = -1.0
        else:
            perm[d - 16, d] = 1.0
    sel = np.zeros((2, 256), np.float32)
    sel[0, 0:128] = 1.0
    sel[1, 128:256] = 1.0
    return ident, bones, perm, sel


_NC_CACHE = {}


def kernel(x_prompt, x_sample, cache_k, cache_v, c, c_ctx, w_ada, b_ada, norm1_w, norm2_w, w_in,
           q_norm_w, k_norm_w, lambda_q1, lambda_k1, lambda_q2, lambda_k2, subln_w, sgu_norm_w,
           w_s, b_s, w_o, w_ff1, w_ff2):
    f = lambda a: np.ascontiguousarray(np.asarray(a, dtype=np.float32))
    x_prompt, x_sample, cache_k, cache_v, c, c_ctx = map(f, (x_prompt, x_sample, cache_k, cache_v, c, c_ctx))
    ident, bones, perm, sel = _consts()
    col = lambda v: np.ascontiguousarray(f(v).reshape(16, 128).T)
    shared = {
        "w_ada": f(w_ada)[0], "bada2": np.ascontiguousarray(np.stack([f(b_ada)[0]] * 2)),
        "nw1": col(norm1_w[0]), "nw2": col(norm2_w[0]), "w_in": f(w_in)[0],
        "qkw": np.ascontiguousarray(np.stack([np.tile(f(q_norm_w)[0], 2), np.tile(f(k_norm_w)[0], 2)], axis=1)),
        "lamv": np.ascontiguousarray(np.broadcast_to(np.concatenate([f(lambda_q1)[0], f(lambda_k1)[0], f(lambda_q2)[0], f(lambda_k2)[0]])[None, :], (128, 256))),
        "sublnw": np.ascontiguousarray(np.broadcast_to(f(subln_w)[0][None, :], (128, 128))),
        "sgunw": np.ascontiguousarray(np.broadcast_to(f(sgu_norm_w)[0][None, :], (128, 1024))),
        "wsT": np.ascontiguousarray(f(w_s)[0].transpose(2, 0, 1).reshape(128, 1024)),
        "bsr": np.ascontiguousarray(f(b_s)[0].reshape(1, 1024)),
        "w_o": f(w_o)[0], "w_ff1": f(w_ff1)[0], "w_ff2": f(w_ff2)[0],
        "c_ident": ident, "c_bones": bones, "c_perm": perm, "c_sel": sel,
    }
    in_maps = []
    for i in range(NCORES):
        b, s = i // 2, i % 2
        own = np.arange(s * 1024, (s + 1) * 1024)
        oth = np.arange((1 - s) * 1024, (2 - s) * 1024)
        order = np.concatenate([own, oth])
        cos, sin = _rope_tables(order)
        cvec = np.stack([c_ctx, c[b]], axis=0)
        cT = np.ascontiguousarray(cvec.reshape(2, 16, 128).transpose(2, 1, 0).reshape(128, 32))
        m = dict(shared)
        m.update({
            "xp": np.ascontiguousarray(x_prompt[2 * i:2 * i + 2].reshape(512, D)),
            "xs": np.ascontiguousarray(x_sample[b][order]),
            "ck": np.ascontiguousarray(cache_k[b, 0].reshape(256, 1024)),
            "cv": np.ascontiguousarray(cache_v[b, 0].reshape(256, 1024)),
            "cT": cT, "c_cos": cos, "c_sin": sin,
        })
        in_maps.append(m)
    if "nc" not in _NC_CACHE:
        _NC_CACHE["nc"] = build_program()
    res = run_bass_kernel_spmd(_NC_CACHE["nc"], in_maps, core_ids=list(range(NCORES)))
    outs = res.results
    y_p = np.zeros((16, 256, D), np.float32)
    y_s = np.zeros((4, 2048, D), np.float32)
    s_k = np.zeros((16, 1, 256, 8, 2, 64), np.float32)
    s_v = np.zeros((16, 1, 256, 8, 128), np.float32)
    for i in range(NCORES):
        b, s = i // 2, i % 2
        y_p[2 * i:2 * i + 2] = outs[i]["yp"].reshape(2, 256, D)
        y_s[b, s * 1024:(s + 1) * 1024] = outs[i]["ys"]
        s_k[2 * i:2 * i + 2, 0] = outs[i]["sk"].reshape(2, 256, 8, 2, 64)
        s_v[2 * i:2 * i + 2, 0] = outs[i]["sv"].reshape(2, 256, 8, 128)
    return (y_p, y_s, s_k, s_v)
```
